# Optimizing a Trainium2 kernel written in Bass

```python
import math, functools
import jax, jax.numpy as jnp
from jax import lax
import numpy as np

D_MODEL = 1024
BATCH = 4
SEQ = 4096
DEPTH = 1
DEC_BATCH = 128
DEC_SEQ = 8
PAST_LEN = 8192
PAGE_SIZE = 128

N_META = 16
MIX_W = D_MODEL
S5_W = MIX_W // 2
S5_GROUP = 16
S5_GROUPS = S5_W // S5_GROUP
S5_STATE = 64
MLA_HEADS = 4
QK_NOPE = 128
QK_ROPE = 64
V_HEAD = 128
Q_RANK = 384
KV_RANK = 256
D_FF = 256 * ((8 * D_MODEL // 3 + 255) // 256)
CONV_W = 3
ROPE_BASE = 10000.0
Q_BLOCK = 128
EPS = 1e-6
IN_W = S5_W + Q_RANK + KV_RANK + QK_ROPE
ATTN_SCALE = 1.0 / math.sqrt(QK_NOPE + QK_ROPE)
N_PAGES = PAST_LEN // PAGE_SIZE
N_POOL = (DEC_BATCH * N_PAGES * 5) // 4

kernel_name = "hymba_s5_mla_convffn_step"

F32 = jnp.float32


def rmsnorm(x, g):
    x32 = x.astype(F32)
    y = x32 * lax.rsqrt(jnp.mean(x32 * x32, axis=-1, keepdims=True) + EPS)
    return (y * g.astype(F32)).astype(x.dtype)


def rope(x, pos):
    half = QK_ROPE // 2
    inv = ROPE_BASE ** (-jnp.arange(half, dtype=F32) / half)
    ang = pos.astype(F32)[:, None] * inv[None, :]
    ang = ang.reshape(ang.shape[:1] + (1,) * (x.ndim - 3) + (half,))
    cos, sin = jnp.cos(ang), jnp.sin(ang)
    x32 = x.astype(F32)
    x1, x2 = x32[..., :half], x32[..., half:]
    return jnp.concatenate([x1 * cos - x2 * sin, x1 * sin + x2 * cos], axis=-1).astype(x.dtype)


def project_mixers(xn, pos, lw):
    z = xn @ lw["w_in"]
    u, c_q, c_kv, k_r = jnp.split(z, [S5_W, S5_W + Q_RANK, S5_W + Q_RANK + KV_RANK], axis=-1)
    b, t = xn.shape[:2]
    q = (rmsnorm(c_q, lw["g_q"]) @ lw["w_uq"]).reshape(b, t, MLA_HEADS, QK_NOPE + QK_ROPE)
    q_lat = jnp.einsum("bthn,rhn->bthr", q[..., :QK_NOPE], lw["w_uk"])
    q_rope = rope(q[..., QK_NOPE:], pos)
    ckv = rmsnorm(c_kv, lw["g_kv"])
    kr = rope(k_r, pos)
    return u, q_lat, q_rope, ckv, kr


def s5_discretise(a_re, a_im, log_dt, b_re, b_im):
    dt = jnp.exp(log_dt.astype(F32))[:, None]
    ar, ai = a_re.astype(F32), a_im.astype(F32)
    mag = jnp.exp(dt * ar)
    abar_re, abar_im = mag * jnp.cos(dt * ai), mag * jnp.sin(dt * ai)
    num_re, num_im = abar_re - 1.0, abar_im
    den = ar * ar + ai * ai
    f_re = (num_re * ar + num_im * ai) / den
    f_im = (num_im * ar - num_re * ai) / den
    br, bi = b_re.astype(F32), b_im.astype(F32)
    bbar_re = f_re[..., None] * br - f_im[..., None] * bi
    bbar_im = f_re[..., None] * bi + f_im[..., None] * br
    return abar_re, abar_im, bbar_re, bbar_im


def _ssm_combine(e1, e2):
    a1r, a1i, b1r, b1i = e1
    a2r, a2i, b2r, b2i = e2
    return (a2r * a1r - a2i * a1i,
            a2r * a1i + a2i * a1r,
            a2r * b1r - a2i * b1i + b2r,
            a2r * b1i + a2i * b1r + b2i)


def s5_mixer(u, h0_re, h0_im, lw):
    b, t, _ = u.shape
    u32 = u.astype(F32)
    ug = u32.reshape(b, t, S5_GROUPS, S5_GROUP)
    abr, abi, bbr, bbi = s5_discretise(lw["s5_a_re"], lw["s5_a_im"], lw["s5_log_dt"], lw["s5_b_re"], lw["s5_b_im"])
    bu_re = jnp.einsum("btgh,gph->tbgp", ug, bbr)
    bu_im = jnp.einsum("btgh,gph->tbgp", ug, bbi)
    h0r, h0i = h0_re.astype(F32), h0_im.astype(F32)
    bu_re = bu_re.at[0].add(abr * h0r - abi * h0i)
    bu_im = bu_im.at[0].add(abr * h0i + abi * h0r)
    a_re = jnp.broadcast_to(abr, (t, 1) + abr.shape)
    a_im = jnp.broadcast_to(abi, (t, 1) + abi.shape)
    _, _, h_re, h_im = lax.associative_scan(_ssm_combine, (a_re, a_im, bu_re, bu_im), axis=0)
    y = (jnp.einsum("tbgp,ghp->btgh", h_re, lw["s5_c_re"].astype(F32))
         - jnp.einsum("tbgp,ghp->btgh", h_im, lw["s5_c_im"].astype(F32)))
    y = y.reshape(b, t, S5_W) + lw["s5_d"].astype(F32) * u32
    y = jax.nn.gelu(y)
    y = y * jax.nn.sigmoid(y @ lw["w_glu"].astype(F32))
    return y.astype(u.dtype), h_re[-1], h_im[-1]


def attend(q_lat, q_rope, q_pos, ckv, kr, k_pos):
    s = (jnp.einsum("bqhr,bkr->bhqk", q_lat, ckv).astype(F32)
         + jnp.einsum("bqhe,bke->bhqk", q_rope, kr).astype(F32)) * ATTN_SCALE
    s = jnp.where(k_pos[None, :] <= q_pos[:, None], s, -jnp.inf)
    p = jax.nn.softmax(s, axis=-1).astype(ckv.dtype)
    return jnp.einsum("bhqk,bkr->bqhr", p, ckv)


def prompt_attention(q_lat, q_rope, ckv, kr, pos):
    b = q_lat.shape[0]
    o_meta = attend(q_lat[:, :N_META], q_rope[:, :N_META], pos[:N_META], ckv, kr, pos)
    nb = (q_lat.shape[1] - N_META) // Q_BLOCK

    def blocks(a):
        return a[:, N_META:].reshape((b, nb, Q_BLOCK) + a.shape[2:]).swapaxes(0, 1)

    def one(args):
        ql, qr, qp = args
        return attend(ql, qr, qp, ckv, kr, pos)

    o = lax.map(one, (blocks(q_lat), blocks(q_rope), pos[N_META:].reshape(nb, Q_BLOCK)))
    o = o.swapaxes(0, 1).reshape((b, nb * Q_BLOCK) + o.shape[3:])
    return jnp.concatenate([o_meta, o], axis=1)


def sample_attention(q_lat, q_rope, ckv, kr, pos, past_ckv, past_kr):
    n_past = past_ckv.shape[1]
    s_past = (jnp.einsum("bqhr,bkr->bhqk", q_lat, past_ckv).astype(F32)
              + jnp.einsum("bqhe,bke->bhqk", q_rope, past_kr).astype(F32)) * ATTN_SCALE
    s_new = (jnp.einsum("bqhr,bkr->bhqk", q_lat, ckv).astype(F32)
             + jnp.einsum("bqhe,bke->bhqk", q_rope, kr).astype(F32)) * ATTN_SCALE
    s_new = jnp.where(pos[None, :] <= pos[:, None], s_new, -jnp.inf)
    p = jax.nn.softmax(jnp.concatenate([s_past, s_new], axis=-1), axis=-1).astype(ckv.dtype)
    return (jnp.einsum("bhqk,bkr->bqhr", p[..., :n_past], past_ckv)
            + jnp.einsum("bhqk,bkr->bqhr", p[..., n_past:], ckv))


def conv_ffn(xn, conv_prev, lw):
    gate = xn @ lw["w_gate"]
    up = xn @ lw["w_up"]
    t = xn.shape[1]
    padded = jnp.concatenate([conv_prev.astype(gate.dtype), gate], axis=1)
    conv = lw["conv_b"] + sum(lw["conv_w"][k] * padded[:, k:k + t] for k in range(CONV_W))
    h = jax.nn.silu(conv) * up
    return h @ lw["w_down"], padded[:, t:]


def hybrid_layer(x, pos, h0_re, h0_im, conv_prev, attention, lw):
    xn = rmsnorm(x, lw["g_mix"])
    u, q_lat, q_rope, ckv, kr = project_mixers(xn, pos, lw)
    y_s5, h_re, h_im = s5_mixer(u, h0_re, h0_im, lw)
    o_lat = attention(q_lat, q_rope, ckv, kr, pos)
    y_mla = jnp.einsum("bthr,rhv->bthv", o_lat, lw["w_uv"]).reshape(x.shape[0], x.shape[1], MLA_HEADS * V_HEAD)
    x = x + jnp.concatenate([y_s5, y_mla], axis=-1) @ lw["w_out"]
    f, conv_new = conv_ffn(rmsnorm(x, lw["g_ffn"]), conv_prev, lw)
    x = x + f
    return x, ckv, kr, h_re, h_im, conv_new


def setup_inputs(seed: int = 0) -> dict:
    key = jax.random.key(seed)
    ks = jax.random.split(key, 40)

    def nrm(k, shape, scale=1.0):
        return jax.random.normal(k, shape, F32) * scale

    def gain(k, shape):
        return 1.0 + 0.02 * jax.random.normal(k, shape, F32)

    page_table = jax.random.permutation(ks[7], N_POOL)[:DEC_BATCH * N_PAGES].reshape(DEC_BATCH, N_PAGES).astype(jnp.int32)
    a_im = jnp.pi * jnp.arange(S5_STATE, dtype=F32)
    return {
        "x_prompt": nrm(ks[0], (BATCH, SEQ, D_MODEL)),
        "x_sample": nrm(ks[1], (DEC_BATCH, DEC_SEQ, D_MODEL)),
        "cache_ckv": nrm(ks[2], (DEPTH, N_POOL, PAGE_SIZE, KV_RANK)),
        "cache_kr": nrm(ks[3], (DEPTH, N_POOL, PAGE_SIZE, QK_ROPE)),
        "state_s5_re": nrm(ks[4], (DEPTH, DEC_BATCH, S5_GROUPS, S5_STATE), 0.1),
        "state_s5_im": nrm(ks[5], (DEPTH, DEC_BATCH, S5_GROUPS, S5_STATE), 0.1),
        "state_conv": nrm(ks[6], (DEPTH, DEC_BATCH, CONV_W - 1, D_FF)),
        "page_table": page_table,
        "meta_tokens": nrm(ks[8], (N_META, D_MODEL)),
        "g_mix": gain(ks[9], (DEPTH, D_MODEL)),
        "w_in": nrm(ks[10], (DEPTH, D_MODEL, IN_W), D_MODEL ** -0.5),
        "g_q": gain(ks[11], (DEPTH, Q_RANK)),
        "w_uq": nrm(ks[12], (DEPTH, Q_RANK, MLA_HEADS * (QK_NOPE + QK_ROPE)), Q_RANK ** -0.5),
        "g_kv": gain(ks[13], (DEPTH, KV_RANK)),
        "w_uk": nrm(ks[14], (DEPTH, KV_RANK, MLA_HEADS, QK_NOPE), KV_RANK ** -0.5),
        "w_uv": nrm(ks[15], (DEPTH, KV_RANK, MLA_HEADS, V_HEAD), KV_RANK ** -0.5),
        "s5_a_re": -0.5 + nrm(ks[16], (DEPTH, S5_GROUPS, S5_STATE), 0.01),
        "s5_a_im": a_im + nrm(ks[17], (DEPTH, S5_GROUPS, S5_STATE), 0.01),
        "s5_log_dt": jax.random.uniform(ks[18], (DEPTH, S5_GROUPS), F32, math.log(1e-3), math.log(1e-1)),
        "s5_b_re": nrm(ks[19], (DEPTH, S5_GROUPS, S5_STATE, S5_GROUP), (2 * S5_GROUP) ** -0.5),
        "s5_b_im": nrm(ks[20], (DEPTH, S5_GROUPS, S5_STATE, S5_GROUP), (2 * S5_GROUP) ** -0.5),
        "s5_c_re": nrm(ks[21], (DEPTH, S5_GROUPS, S5_GROUP, S5_STATE), S5_STATE ** -0.5),
        "s5_c_im": nrm(ks[22], (DEPTH, S5_GROUPS, S5_GROUP, S5_STATE), S5_STATE ** -0.5),
        "s5_d": nrm(ks[23], (DEPTH, S5_W)),
        "w_glu": nrm(ks[24], (DEPTH, S5_W, S5_W), S5_W ** -0.5),
        "w_out": nrm(ks[25], (DEPTH, MIX_W, D_MODEL), MIX_W ** -0.5),
        "g_ffn": gain(ks[26], (DEPTH, D_MODEL)),
        "w_gate": nrm(ks[27], (DEPTH, D_MODEL, D_FF), D_MODEL ** -0.5),
        "w_up": nrm(ks[28], (DEPTH, D_MODEL, D_FF), D_MODEL ** -0.5),
        "conv_w": nrm(ks[29], (DEPTH, CONV_W, D_FF), CONV_W ** -0.5),
        "conv_b": nrm(ks[30], (DEPTH, D_FF), 0.01),
        "w_down": nrm(ks[31], (DEPTH, D_FF, D_MODEL), D_FF ** -0.5),
        "g_final": gain(ks[32], (D_MODEL,)),
    }


def reference(x_prompt, x_sample, cache_ckv, cache_kr, state_s5_re, state_s5_im, state_conv, page_table,
              meta_tokens, g_mix, w_in, g_q, w_uq, g_kv, w_uk, w_uv, s5_a_re, s5_a_im, s5_log_dt,
              s5_b_re, s5_b_im, s5_c_re, s5_c_im, s5_d, w_glu, w_out, g_ffn, w_gate, w_up, conv_w, conv_b,
              w_down, g_final):
    b = x_prompt.shape[0]
    db = x_sample.shape[0]
    t_p = N_META + x_prompt.shape[1]
    t_s = x_sample.shape[1]
    n_past = page_table.shape[1] * PAGE_SIZE

    hp = jnp.concatenate([jnp.broadcast_to(meta_tokens.astype(x_prompt.dtype)[None], (b, N_META, D_MODEL)), x_prompt], axis=1)
    hs = x_sample
    pos_p = jnp.arange(t_p, dtype=jnp.int32)
    pos_s = n_past + jnp.arange(t_s, dtype=jnp.int32)
    h0_zero = jnp.zeros((b, S5_GROUPS, S5_STATE), F32)
    conv_zero = jnp.zeros((b, CONV_W - 1, D_FF), x_prompt.dtype)

    ckv_p_l, kr_p_l, sre_p_l, sim_p_l, conv_p_l = [], [], [], [], []
    ckv_s_l, kr_s_l, sre_s_l, sim_s_l, conv_s_l = [], [], [], [], []
    for l in range(DEPTH):
        lw = dict(g_mix=g_mix[l], w_in=w_in[l], g_q=g_q[l], w_uq=w_uq[l], g_kv=g_kv[l], w_uk=w_uk[l],
                  w_uv=w_uv[l], s5_a_re=s5_a_re[l], s5_a_im=s5_a_im[l], s5_log_dt=s5_log_dt[l],
                  s5_b_re=s5_b_re[l], s5_b_im=s5_b_im[l], s5_c_re=s5_c_re[l], s5_c_im=s5_c_im[l],
                  s5_d=s5_d[l], w_glu=w_glu[l], w_out=w_out[l], g_ffn=g_ffn[l], w_gate=w_gate[l],
                  w_up=w_up[l], conv_w=conv_w[l], conv_b=conv_b[l], w_down=w_down[l])
        hp, ckv_p, kr_p, sre_p, sim_p, conv_p = hybrid_layer(hp, pos_p, h0_zero, h0_zero, conv_zero, prompt_attention, lw)
        past_ckv = cache_ckv[l, page_table].reshape(db, n_past, KV_RANK)
        past_kr = cache_kr[l, page_table].reshape(db, n_past, QK_ROPE)
        attn_s = functools.partial(sample_attention, past_ckv=past_ckv, past_kr=past_kr)
        hs, ckv_s, kr_s, sre_s, sim_s, conv_s = hybrid_layer(hs, pos_s, state_s5_re[l], state_s5_im[l], state_conv[l], attn_s, lw)
        ckv_p_l.append(ckv_p); kr_p_l.append(kr_p); sre_p_l.append(sre_p); sim_p_l.append(sim_p); conv_p_l.append(conv_p)
        ckv_s_l.append(ckv_s); kr_s_l.append(kr_s); sre_s_l.append(sre_s); sim_s_l.append(sim_s); conv_s_l.append(conv_s)

    y_prompt = rmsnorm(hp[:, N_META:], g_final)
    y_sample = rmsnorm(hs, g_final)
    return (y_prompt, y_sample,
            jnp.stack(ckv_p_l), jnp.stack(kr_p_l), jnp.stack(sre_p_l), jnp.stack(sim_p_l), jnp.stack(conv_p_l),
            jnp.stack(ckv_s_l), jnp.stack(kr_s_l), jnp.stack(sre_s_l), jnp.stack(sim_s_l), jnp.stack(conv_s_l))
```

```python
import math
from contextlib import ExitStack
import numpy as np
import ml_dtypes
import concourse.bass as bass
import concourse.mybir as mybir
from concourse.bass_utils import run_bass_kernel_spmd

F32 = mybir.dt.float32
BF16 = mybir.dt.bfloat16
I32 = mybir.dt.int32
ALU = mybir.AluOpType
AF = mybir.ActivationFunctionType
AX = mybir.AxisListType
NPBF = ml_dtypes.bfloat16

D = 1024; T = 4112; NTA = 34; TPAD = 4224
S5W = 512; QR = 384; KVR = 256; ROPE = 64; INW = 1216
NPOOL = 10240; NPAGES = 64
DFF = 2816; NFT = 22
EPS = 1e-6
SCALE = 1.0 / math.sqrt(192.0)
NEG = -30000.0
NSLOT = 16
NOWN = NSLOT * 128 + 64
EP = 12000
_DBG = {}
_STOP_AFTER = None


class _NeedDefer(Exception):
    pass


class _Rec:
    def __init__(self):
        self.call = None

    def __getattr__(self, name):
        if name == "value_load":
            raise _NeedDefer()

        def f(*args, **kwargs):
            self.call = (name, args, kwargs)
            return self
        return f


def _record(fn):
    r = _Rec()
    try:
        fn(r)
    except _NeedDefer:
        return fn
    assert r.call is not None
    return r.call


class Prog:
    ENG = ("pe", "act", "dve", "pool", "sp")

    def __init__(self, nc, es):
        self.nc, self.es = nc, es
        self.q = {e: [] for e in self.ENG}
        self.cnt = {e: 0 for e in self.ENG}
        self.sems = {}
        self.waited = {}
        self.lw = {}
        self.rd = {}
        self.dcnt = {}
        self.dlast = {}

    def sem(self, key):
        if key not in self.sems:
            self.sems[key] = self.es.enter_context(self.nc.semaphore("s%d" % len(self.sems)))
        return self.sems[key]

    def _resolve(self, e, deps):
        waits = []
        for tok in deps:
            if tok is None:
                continue
            kind, key, val = tok
            if kind == "e":
                if key[0] == e and e == "pe":
                    continue
            else:
                val = self.dlast[key]
            if self.waited.get((e, key), 0) >= val:
                continue
            self.waited[(e, key)] = val
            waits.append((self.sem(key), val))
        return waits

    def _deps(self, r, w):
        deps = []
        for x in r:
            deps.append(self.lw.get(x))
        for x in w:
            deps.append(self.lw.get(x))
            deps.extend(self.rd.get(x, {}).values())
        return deps

    def _commit(self, tok, r, w, rkey):
        for x in w:
            self.lw[x] = tok
            self.rd[x] = {}
        for x in r:
            self.rd.setdefault(x, {})[rkey] = tok

    def op(self, e, fn, r=(), w=()):
        w = list(w) + [x for x in r if isinstance(x, str) and x.startswith("ps_") and x not in w]
        waits = self._resolve(e, self._deps(r, w))
        n = self.cnt[e]
        self.cnt[e] += 1
        key = (e, n // EP)
        self.q[e].append((waits, _record(fn), self.sem(key), 1))
        self._commit(("e", key, n % EP + 1), r, w, e)

    def dma(self, e, fn, r=(), w=(), key="d"):
        waits = self._resolve(e, self._deps(r, w))
        c = self.dcnt.get(key, 0)
        self.dcnt[key] = c + 1
        k2 = ("dma", key, c // 1500)
        val = (c % 1500 + 1) * 16
        self.dlast[k2] = val
        self.q[e].append((waits, _record(fn), self.sem(k2), 16))
        self._commit(("d", k2, val), r, w, ("dma", key))

    def barrier(self):
        toks = []
        for e in self.ENG:
            n = self.cnt[e]
            if n > 0:
                toks.append(("e", (e, (n - 1) // EP), (n - 1) % EP + 1))
        for k2, v in self.dlast.items():
            toks.append(("d", k2, v))
        for e in self.ENG:
            waits = self._resolve(e, toks)
            if waits:
                self.q[e].append((waits, None, None, 0))

    def flush(self, final=False):
        def run(eng, lst):
            for waits, fn, sem, inc in lst:
                for s, v in waits:
                    eng.wait_ge(s, v)
                if fn is not None:
                    if isinstance(fn, tuple):
                        ins = getattr(eng, fn[0])(*fn[1], **fn[2])
                    else:
                        ins = fn(eng)
                    ins.then_inc(sem, inc)

        q = self.q
        self.q = {e: [] for e in self.ENG}
        with self.nc.Block() as block:
            @block.tensor
            def _(e):
                run(e, q["pe"])

            @block.scalar
            def _(e):
                run(e, q["act"])

            @block.vector
            def _(e):
                run(e, q["dve"])

            @block.gpsimd
            def _(e):
                run(e, q["pool"])

            @block.sync
            def _(e):
                run(e, q["sp"])
                if final:
                    for k2, v in self.dlast.items():
                        e.wait_ge(self.sem(k2), v)


def build_program(npool=NPOOL):
    nc = bass.Bass("TRN2", target_bir_lowering=False)

    def din(name, shape, dt=F32):
        return nc.dram_tensor(name, list(shape), dt, kind="ExternalInput").ap()

    def dout(name, shape, dt=F32):
        return nc.dram_tensor(name, list(shape), dt, kind="ExternalOutput").ap()

    def dscr(name, shape, dt):
        return nc.dram_tensor(name, list(shape), dt, kind="Internal").ap()

    xall = din("xall", [NTA * 128, D])
    xown = din("xown", [NOWN, D])
    ropeall = din("ropeall", [NTA * 128, 64])
    ropeown = din("ropeown", [NOWN + 128, 64])
    maskab = din("maskab", [128, 4 * 128], BF16)
    masksp = din("masksp", [64, TPAD], BF16)
    masksm = din("masksm", [128, 4 * 128], BF16)
    halov = din("halov", [128, 34])
    idxown = din("idxown", [128, 17], I32)
    tbcol = din("tbcol", [128, 1])
    ptab = din("ptab", [1, 16 * NPAGES], I32)
    identd = din("identd", [128, 128])
    tmaskd = din("tmaskd", [128, 128])
    ddiagd = din("ddiagd", [128, 128])
    cckv = din("cache_ckv", [npool, 128 * KVR])
    ckr = din("cache_kr", [npool, 128 * ROPE])
    s5re = din("st_re", [16, 2048])
    s5im = din("st_im", [16, 2048])
    stconv = din("st_conv", [32, DFF])
    g_mix = din("g_mix", [1, D]); w_in = din("w_in", [D, INW]); g_q = din("g_q", [1, QR])
    w_uq = din("w_uq", [QR, 768]); g_kv = din("g_kv", [1, KVR])
    w_uk = din("w_uk", [KVR, 512]); w_uv = din("w_uv", [KVR, 512])
    a_re = din("s5_a_re", [32, 64]); a_im = din("s5_a_im", [32, 64]); log_dt = din("s5_log_dt", [1, 32])
    b_re = din("s5_b_re", [32, 64, 16]); b_im = din("s5_b_im", [32, 64, 16])
    c_re = din("s5_c_re", [512, 64]); c_im = din("s5_c_im", [512, 64])
    s5_d = din("s5_d", [1, 512]); w_glu = din("w_glu", [512, 512]); w_out = din("w_out", [D, D])
    g_ffn = din("g_ffn", [1, D]); w_gate = din("w_gate", [D, DFF]); w_up = din("w_up", [D, DFF])
    conv_w = din("conv_w", [3, DFF]); conv_b = din("conv_b", [1, DFF]); w_down = din("w_down", [DFF, D])
    g_final = din("g_final", [1, D])

    o_y = dout("o_y", [NOWN, D]); o_ys = dout("o_ys", [128, D])
    o_ckv = dout("o_ckv", [NTA * 128, KVR]); o_kr = dout("o_kr", [NTA * 128, ROPE])
    o_s5p = dout("o_s5p", [32, 128]); o_s5s = dout("o_s5s", [32, 2048])
    o_cvp = dout("o_cvp", [2, DFF]); o_cvs = dout("o_cvs", [32, DFF])

    wg_t = dscr("wg_t", [NFT, 128, 1024], BF16)
    wu_t = dscr("wu_t", [NFT, 128, 1024], BF16)
    wd_t = dscr("wd_t", [NFT, 128, 1024], BF16)
    ys_d = dscr("ys_d", [NTA * 128, 512], BF16)

    with ExitStack() as es:
        P = Prog(nc, es)

        def sb(name, shape, dt, scope=es):
            return scope.enter_context(nc.sbuf_tensor(name, list(shape), dt))

        def pst(name, shape, dt):
            return es.enter_context(nc.psum_tensor(name, list(shape), dt))

        ps_mm = [pst("ps_mm%d" % i, [128, 512], F32) for i in range(2)]
        ps_tr = [pst("ps_tr%d" % i, [128, 1024], BF16) for i in range(2)]
        ps_trf = pst("ps_trf", [128, 512], F32)
        ps_qk = [pst("ps_qk%d" % i, [128, 512], F32) for i in range(2)]
        ps_pv = pst("ps_pv", [128, 512], F32)
        ps_tr1f = ps_tr[1][:, :].bitcast(F32)
        ps_trf_b = ps_trf[:, :].bitcast(BF16)
        qk_banks = [(ps_qk[0], "ps_qk0"), (ps_qk[1], "ps_qk1"), (ps_mm[0], "ps_mm0"), (ps_mm[1], "ps_mm1")]
        tr_banks3 = [(ps_tr[0], "ps_tr0"), (ps_tr[1], "ps_tr1"), (ps_trf_b, "ps_trf")]

        ident = sb("ident", [128, 128], BF16)
        identf = sb("identf", [128, 128], F32)
        gmix = sb("gmix", [128, D], F32); gffn = sb("gffn", [128, D], F32); gfin = sb("gfin", [128, D], F32)
        gq = sb("gq", [128, QR], F32); gkv = sb("gkv", [128, KVR], F32)
        epsb = sb("epsb", [128, 1], F32)
        KT = sb("KT", [128, 3, NTA * 128], BF16)
        V = sb("V", [128, NTA, KVR], BF16)
        stat = sb("stat", [128, 16], F32)
        scr = sb("scr", [128, D], F32)
        ropet = sb("ropet", [128, 4, 4, 32], F32)
        oTs = sb("oTs", [128, 4, 2, 128], BF16)
        rot = {}

        def nxt(k, n=2):
            rot[k] = (rot.get(k, 0) + 1) % n
            return rot[k]

        def evac_eng():
            return ("act", "dve")[nxt("ev")]

        def copy(e, out, in_, r, w):
            if e == "act":
                P.op("act", lambda g: g.copy(out=out, in_=in_), r=r, w=w)
            else:
                P.op(e, lambda g: g.tensor_copy(out=out, in_=in_), r=r, w=w)

        def load(q, out, in_, w, key, r=(), slow=False):
            if slow:
                P.dma(q, lambda g: g.dma_start(out=out, in_=in_, allow_slow_non_contiguous=True), r=r, w=w, key=key)
            else:
                P.dma(q, lambda g: g.dma_start(out=out, in_=in_), r=r, w=w, key=key)

        def tt(o, a, b, op, r, w, e="dve"):
            P.op(e, lambda g: g.tensor_tensor(out=o, in0=a, in1=b, op=op), r=r, w=w)

        def mm_bank():
            i = nxt("mm")
            return ps_mm[i], "ps_mm%d" % i

        def tr_bank():
            i = nxt("tr")
            return ps_tr[i], "ps_tr%d" % i

        load("pool", ident[:], identd, ["ident"], "c0")
        load("sp", identf[:], identd, ["identf"], "c1")
        for nm, t, src in (("gmix", gmix, g_mix), ("gffn", gffn, g_ffn), ("gfin", gfin, g_final), ("gq", gq, g_q), ("gkv", gkv, g_kv)):
            load("sp", t[:], src.partition_broadcast(128), [nm], "c1")
        P.op("pool", lambda g: g.memset(epsb[:], EPS), w=["epsb"])

        def rstd_from_ss(col, n, rows=128):
            c = stat[0:rows, col:col + 1]
            nm = "st%d" % col
            P.op("act", lambda g: g.activation(out=c, in_=c, func=AF.Sqrt, scale=1.0 / n, bias=epsb[0:rows, 0:1]), r=[nm, "epsb"], w=[nm])
            P.op("dve", lambda g: g.reciprocal(out=c, in_=c), r=[nm], w=[nm])

        def transposes(src_aps, dst_fn, rs, f32=False):
            if f32:
                pt, res = ps_trf, "ps_trf"
            else:
                pt, res = tr_bank()
            off = 0
            for a in src_aps:
                p, f = a.shape[0], a.shape[1]
                idn = (identf if f32 else ident)[0:p, 0:p]
                o = pt[0:f, off:off + p]
                P.op("pe", lambda g, o=o, a=a, idn=idn: g.transpose(out=o, in_=a, identity=idn), r=list(rs) + ["ident", "identf"], w=[res])
                off += p
            dst_fn(pt, res)

        def rmsnorm_tile(xt, xres, gb, gres, out_bf, ores, n=D, col=0, rows=128):
            nm = "st%d" % col
            P.op("act", lambda g: g.activation(out=scr[0:rows, 0:n], in_=xt, func=AF.Square, accum_out=stat[0:rows, col:col + 1]), r=[xres], w=["scr", nm])
            rstd_from_ss(col, n, rows)
            P.op("dve", lambda g: g.scalar_tensor_tensor(out=out_bf, in0=xt, scalar=stat[0:rows, col:col + 1], in1=gb, op0=ALU.mult, op1=ALU.mult),
                 r=[xres, nm, gres], w=[ores])

        def rope_tile(raw, rres, cos, sin, tres, out, ores, nh, eng="pool"):
            cb = cos.unsqueeze(1).to_broadcast([128, nh, 32]) if nh > 1 else cos.unsqueeze(1)
            sbb = sin.unsqueeze(1).to_broadcast([128, nh, 32]) if nh > 1 else sin.unsqueeze(1)
            x1_ = raw[:, :, 0:32]; x2_ = raw[:, :, 32:64]
            t = ropet
            ops = [(t[:, 0, 0:nh, :], x1_, cb), (t[:, 1, 0:nh, :], x2_, sbb), (t[:, 2, 0:nh, :], x1_, sbb), (t[:, 3, 0:nh, :], x2_, cb)]
            for i, (o, a, b) in enumerate(ops):
                tt(o, a, b, ALU.mult, list(rres) + [tres], ["ropet%d" % i], e=eng)
            tt(out[:, :, 0:32], t[:, 0, 0:nh, :], t[:, 1, 0:nh, :], ALU.subtract, ["ropet0", "ropet1"], [ores + "a"], e=eng)
            tt(out[:, :, 32:64], t[:, 2, 0:nh, :], t[:, 3, 0:nh, :], ALU.add, ["ropet2", "ropet3"], [ores + "b"], e=eng)

        with ExitStack() as sa:
            w_inA = sb("w_inA", [128, 8, 832], BF16, sa)
            for kt in range(8):
                load("pool", w_inA[:, kt, 0:512], w_in[kt * 128:(kt + 1) * 128, 0:512], ["w_inA"], "wA")
                load("pool", w_inA[:, kt, 512:832], w_in[kt * 128:(kt + 1) * 128, 896:1216], ["w_inA"], "wA")
            ropeA = [sb("ropeA%d" % i, [128, 64], F32, sa) for i in range(2)]

            Wg = sb("Wg", [128, 32, 128], BF16, sa)
            Tg = sb("Tg", [128, 32, 128], BF16, sa)
            Cw = sb("Cw", [128, 2, 32, 128], BF16, sa)
            P.op("pool", lambda g: g.memset(Cw[:], 0.0), w=["Cw0", "Cw1"])
            Kc = sb("Kc", [128, 2, 2, 16], F32, sa)
            h0L = sb("h0L", [128, 2, 16, 16], F32, sa)

            with ExitStack() as sp5:
                def s5sb(name, shape, dt=F32):
                    return sb(name, shape, dt, sp5)
                ar = s5sb("ar", [128, 32]); ai = s5sb("ai", [128, 32]); ldt = s5sb("ldt", [128, 32])
                for h in range(2):
                    load("sp", ar[h * 64:(h + 1) * 64, :], a_re.rearrange("g p -> p g"), ["ar"], "c1", slow=True)
                    load("sp", ai[h * 64:(h + 1) * 64, :], a_im.rearrange("g p -> p g"), ["ai"], "c1", slow=True)
                load("sp", ldt[:], log_dt.partition_broadcast(128), ["ldt"], "c1")
                Bs = s5sb("Bs", [128, 2, 32, 16])
                for h in range(2):
                    load("sp", Bs[h * 64:(h + 1) * 64, 0, :, :], b_re.rearrange("g p h -> p g h"), ["Bs"], "c1")
                    load("sp", Bs[h * 64:(h + 1) * 64, 1, :, :], b_im.rearrange("g p h -> p g h"), ["Bs"], "c1")
                Cn = s5sb("Cn", [128, 2, 4, 64])
                load("sp", Cn[:, 0, :, :], c_re.rearrange("(q r) p -> r q p", r=128), ["Cn"], "c1")
                load("sp", Cn[:, 1, :, :], c_im.rearrange("(q r) p -> r q p", r=128), ["Cn"], "c1")
                Cn2 = s5sb("Cn2", [128, 2, 4, 2, 64])
                for d in range(2):
                    P.op("pool", lambda g, d=d: g.tensor_copy(out=Cn2[:, :, :, d, :], in_=Cn[:, :, :, :]), r=["Cn"], w=["Cn2%d" % d])
                Cs = s5sb("Cs", [128, 2, 32, 16])
                for t in range(2):
                    srcs = [Cn2[:, t, qd, :, :].rearrange("r d p -> r (d p)") for qd in range(4)]

                    def ev(pt, res, t=t):
                        P.op("dve", lambda g: g.tensor_copy(out=Cs[:, t, :, :].rearrange("p g h -> p (g h)"), in_=pt[:, 0:512]), r=[res], w=["Cs"])
                    transposes(srcs, ev, ["Cn20", "Cn21"], f32=True)
                Dbc = s5sb("Dbc", [128, 32, 16])
                load("sp", Dbc[:].rearrange("p g h -> p (g h)"), s5_d.partition_broadcast(128), ["Dbc"], "c1")
                tmask = s5sb("tmask", [128, 128]); ddiag = s5sb("ddiag", [128, 128])
                load("sp", tmask[:], tmaskd, ["tmask"], "c1")
                load("sp", ddiag[:], ddiagd, ["ddiag"], "c1")

                dt_ = s5sb("dt_", [128, 32]); lam = s5sb("lam", [128, 32]); th = s5sb("th", [128, 32])
                P.op("act", lambda g: g.activation(out=dt_[:], in_=ldt[:], func=AF.Exp), r=["ldt"], w=["dt_"])
                tt(lam[:], dt_[:], ar[:], ALU.mult, ["dt_", "ar"], ["lam"])
                tt(th[:], dt_[:], ai[:], ALU.mult, ["dt_", "ai"], ["th"])
                NPW = 17
                pwr = s5sb("pwr", [128, NPW, 32]); pwi = s5sb("pwi", [128, NPW, 32])
                mag = s5sb("mag", [128, NPW, 32]); ang = s5sb("ang", [128, NPW, 32]); ang2 = s5sb("ang2", [128, NPW, 32])
                kf = s5sb("kf", [128, NPW, 32]); ki = s5sb("ki", [128, NPW, 32], I32)
                TWO_PI = 2.0 * math.pi
                for i in range(NPW):
                    k = i - 7
                    P.op("act", lambda g, i=i, k=k: g.activation(out=mag[:, i, :], in_=lam[:], func=AF.Exp, scale=float(k)), r=["lam"], w=["mag"])
                    P.op("dve", lambda g, i=i, k=k: g.tensor_scalar(out=ang[:, i, :], in0=th[:], scalar1=float(k), scalar2=32 * TWO_PI, op0=ALU.mult, op1=ALU.add),
                         r=["th"], w=["ang"])
                P.op("dve", lambda g: g.tensor_scalar(out=ang2[:], in0=ang[:], scalar1=0.5 * math.pi, scalar2=None, op0=ALU.add), r=["ang"], w=["ang2"])
                for a_, nm in ((ang, "ang"), (ang2, "ang2")):
                    P.op("dve", lambda g, a_=a_: g.tensor_scalar(out=kf[:], in0=a_[:], scalar1=1.0 / TWO_PI, scalar2=None, op0=ALU.mult), r=[nm], w=["kf"])
                    P.op("dve", lambda g: g.tensor_copy(out=ki[:], in_=kf[:]), r=["kf"], w=["ki"])
                    P.op("dve", lambda g: g.tensor_copy(out=kf[:], in_=ki[:]), r=["ki"], w=["kf"])
                    P.op("dve", lambda g, a_=a_: g.scalar_tensor_tensor(out=a_[:], in0=kf[:], scalar=-TWO_PI, in1=a_[:], op0=ALU.mult, op1=ALU.add), r=["kf", nm], w=[nm])
                    P.op("dve", lambda g, a_=a_: g.tensor_single_scalar(out=kf[:], in_=a_[:], scalar=math.pi, op=ALU.is_ge), r=[nm], w=["kf"])
                    P.op("dve", lambda g, a_=a_: g.scalar_tensor_tensor(out=a_[:], in0=kf[:], scalar=-TWO_PI, in1=a_[:], op0=ALU.mult, op1=ALU.add), r=["kf", nm], w=[nm])
                P.op("act", lambda g: g.activation(out=pwi[:], in_=ang[:], func=AF.Sin), r=["ang"], w=["pwi"])
                P.op("act", lambda g: g.activation(out=pwr[:], in_=ang2[:], func=AF.Sin), r=["ang2"], w=["pwr"])
                tt(pwi[:], pwi[:], mag[:], ALU.mult, ["pwi", "mag"], ["pwi"])
                tt(pwr[:], pwr[:], mag[:], ALU.mult, ["pwr", "mag"], ["pwr"])

                def PW(k):
                    return pwr[:, k + 7, :], pwi[:, k + 7, :]

                fr = s5sb("fr", [128, 32]); fi = s5sb("fi", [128, 32]); den = s5sb("den", [128, 32])
                nr = s5sb("nr", [128, 32]); t1 = s5sb("t1", [128, 32]); t2 = s5sb("t2", [128, 32])
                a1r, a1i = PW(1)
                P.op("dve", lambda g: g.tensor_scalar(out=nr[:], in0=a1r, scalar1=-1.0, scalar2=None, op0=ALU.add), r=["pwr"], w=["nr"])
                tt(t1[:], ar[:], ar[:], ALU.mult, ["ar"], ["t1"])
                tt(t2[:], ai[:], ai[:], ALU.mult, ["ai"], ["t2"])
                tt(den[:], t1[:], t2[:], ALU.add, ["t1", "t2"], ["den"])
                P.op("dve", lambda g: g.reciprocal(out=den[:], in_=den[:]), r=["den"], w=["den"])
                tt(t1[:], nr[:], ar[:], ALU.mult, ["nr", "ar", "den"], ["t1"])
                tt(t2[:], a1i, ai[:], ALU.mult, ["pwi", "ai", "den"], ["t2"])
                tt(fr[:], t1[:], t2[:], ALU.add, ["t1", "t2"], ["fr"])
                tt(fr[:], fr[:], den[:], ALU.mult, ["fr", "den"], ["fr"])
                tt(t1[:], a1i, ar[:], ALU.mult, ["pwi", "ar", "fr"], ["t1"])
                tt(t2[:], nr[:], ai[:], ALU.mult, ["nr", "ai", "fr"], ["t2"])
                tt(fi[:], t1[:], t2[:], ALU.subtract, ["t1", "t2"], ["fi"])
                tt(fi[:], fi[:], den[:], ALU.mult, ["fi", "den"], ["fi"])

                def bc16(a):
                    return a.unsqueeze(2).to_broadcast([128, 32, 16])

                cm1 = s5sb("cm1", [128, 32, 16]); cm2 = s5sb("cm2", [128, 32, 16])

                def cmul(o_r, o_i, xr, xi, yr, yi, rx, ry, wname):
                    rr = list(rx) + list(ry)
                    tt(cm1[:], xr, yr, ALU.mult, rr, ["cm1"])
                    tt(cm2[:], xi, yi, ALU.mult, rr, ["cm2"])
                    tt(o_r, cm1[:], cm2[:], ALU.subtract, ["cm1", "cm2"], [wname + "r"])
                    tt(cm1[:], xr, yi, ALU.mult, rr + [wname + "r"], ["cm1"])
                    tt(cm2[:], xi, yr, ALU.mult, rr + [wname + "r"], ["cm2"])
                    tt(o_i, cm1[:], cm2[:], ALU.add, ["cm1", "cm2"], [wname + "i"])

                Bb = s5sb("Bb", [128, 2, 32, 16])
                cmul(Bb[:, 0], Bb[:, 1], bc16(fr[:]), bc16(fi[:]), Bs[:, 0], Bs[:, 1], ["fr", "fi"], ["Bs"], "Bb")
                Xs = s5sb("Xs", [128, 32, 8, 16])
                Ys = s5sb("Ys", [128, 32, 8, 16])
                tr_ = s5sb("tr_", [128, 32, 16]); ti_ = s5sb("ti_", [128, 32, 16])

                def stack_into(dst, rr, ii, rres, wname):
                    P.op("pool", lambda g: g.tensor_copy(out=dst[0:64], in_=rr[0:64]), r=rres, w=[wname + "lo"])
                    P.op("pool", lambda g: g.tensor_copy(out=dst[64:128], in_=ii[64:128]), r=rres, w=[wname + "hi"])

                for j in range(8):
                    pr, pi = PW(7 - j)
                    cmul(tr_[:], ti_[:], bc16(pr), bc16(pi), Bb[:, 0], Bb[:, 1], ["pwr", "pwi"], ["Bbr", "Bbi"], "tW")
                    stack_into(Xs[:, :, j, :], tr_[:], ti_[:], ["tWr", "tWi"], "Xs")
                for g4 in range(8):
                    srcs = [Xs[:, g4 * 4 + q, :, :].rearrange("p j h -> p (j h)") for q in range(4)]

                    def ev(pt, res, g4=g4):
                        P.op("dve", lambda g: g.tensor_copy(out=Wg[:, g4 * 4:(g4 + 1) * 4, :].rearrange("p g s -> p (g s)"), in_=pt[:, 0:512]), r=[res], w=["Wg"])
                    transposes(srcs, ev, ["Xslo", "Xshi"], f32=True)
                for i in range(8):
                    pr, pi = PW(-i)
                    cmul(tr_[:], ti_[:], bc16(pr), bc16(pi), Bb[:, 0], Bb[:, 1], ["pwr", "pwi"], ["Bbr", "Bbi"], "tW")
                    stack_into(Xs[:, :, i, :], tr_[:], ti_[:], ["tWr", "tWi"], "Xs")
                for j in range(8):
                    pr, pi = PW(j)
                    cmul(tr_[:], ti_[:], bc16(pr), bc16(pi), Cs[:, 0], Cs[:, 1], ["pwr", "pwi"], ["Cs"], "tW")
                    P.op("pool", lambda g: g.tensor_scalar(out=ti_[:], in0=ti_[:], scalar1=-1.0, scalar2=None, op0=ALU.mult), r=["tWi"], w=["tWi"])
                    stack_into(Ys[:, :, j, :], tr_[:], ti_[:], ["tWr", "tWi"], "Ys")
                tgt = s5sb("tgt", [128, 4, 128]); dd = s5sb("dd", [128, 8, 16])
                for g4 in range(8):
                    pm, pres = mm_bank()
                    for q in range(4):
                        g_ = g4 * 4 + q
                        P.op("pe", lambda g, g_=g_, q=q, pm=pm: g.matmul(pm[:, q * 128:(q + 1) * 128], lhsT=Xs[:, g_, :, :].rearrange("p j h -> p (j h)"),
                                                                       rhs=Ys[:, g_, :, :].rearrange("p j h -> p (j h)"), start=True, stop=True),
                             r=["Xslo", "Xshi", "Yslo", "Yshi"], w=[pres])
                    tt(tgt[:], pm[:, :].rearrange("p (q s) -> p q s", q=4), tmask[:].unsqueeze(1).to_broadcast([128, 4, 128]), ALU.mult, [pres, "tmask"], ["tgt"])
                    for q in range(4):
                        g_ = g4 * 4 + q
                        tt(dd[:], ddiag[:].rearrange("p (j h) -> p j h", j=8), Dbc[:, g_, :].unsqueeze(1).to_broadcast([128, 8, 16]), ALU.mult, ["ddiag", "Dbc"], ["dd"], e="pool")
                        tt(Tg[:, g_, :], tgt[:, q, :], dd[:].rearrange("p j h -> p (j h)"), ALU.add, ["tgt", "dd"], ["Tg"])
                for j in range(8):
                    pr, pi = PW(j + 1)
                    cmul(tr_[:], ti_[:], bc16(pr), bc16(pi), Cs[:, 0], Cs[:, 1], ["pwr", "pwi"], ["Cs"], "tW")
                    trv = tr_[:].rearrange("p (gq two) h -> p gq two h", two=2)
                    tiv = ti_[:].rearrange("p (gq two) h -> p gq two h", two=2)
                    for h in range(2):
                        lo, hi = h * 64, (h + 1) * 64
                        P.op("pool", lambda g, lo=lo, hi=hi, h=h, j=j, trv=trv: g.tensor_copy(out=Cw[lo:hi, 0, :, j * 16:(j + 1) * 16].rearrange("p (gq two) n -> p gq two n", two=2)[:, :, h, :], in_=trv[lo:hi, :, h, :]),
                             r=["tWr"], w=["Cw%d" % h])
                        P.op("pool", lambda g, lo=lo, hi=hi, h=h, j=j, tiv=tiv: g.tensor_scalar(out=Cw[lo:hi, 1, :, j * 16:(j + 1) * 16].rearrange("p (gq two) n -> p gq two n", two=2)[:, :, h, :], in0=tiv[lo:hi, :, h, :],
                                                                                                scalar1=-1.0, scalar2=None, op0=ALU.mult), r=["tWi"], w=["Cw%d" % h])
                p8r, p8i = PW(8)
                rv = p8r.rearrange("p (gq two) -> p gq two", two=2)
                iv = p8i.rearrange("p (gq two) -> p gq two", two=2)
                for h in range(2):
                    lo, hi = h * 64, (h + 1) * 64
                    P.op("pool", lambda g, lo=lo, hi=hi, h=h: g.tensor_copy(out=Kc[lo:hi, 0, 0, :], in_=rv[lo:hi, :, h]), r=["pwr"], w=["Kc%d" % h])
                    P.op("pool", lambda g, lo=lo, hi=hi, h=h: g.tensor_copy(out=Kc[lo:hi, 1, 1, :], in_=rv[lo:hi, :, h]), r=["pwr"], w=["Kc%d" % h])
                    P.op("pool", lambda g, lo=lo, hi=hi, h=h: g.tensor_copy(out=Kc[lo:hi, 1, 0, :], in_=iv[lo:hi, :, h]), r=["pwi"], w=["Kc%d" % h])
                    P.op("pool", lambda g, lo=lo, hi=hi, h=h: g.tensor_scalar(out=Kc[lo:hi, 0, 1, :], in0=iv[lo:hi, :, h], scalar1=-1.0, scalar2=None, op0=ALU.mult),
                         r=["pwi"], w=["Kc%d" % h])
                P.barrier()
                P.flush()
            with ExitStack() as sp6:
                h0n = sb("h0n", [16, 2, 2048], F32, sp6)
                load("sp", h0n[:, 0, :], s5re, ["h0n"], "c1")
                load("sp", h0n[:, 1, :], s5im, ["h0n"], "c1")
                for t in range(2):
                    for half in range(2):
                        srcs = [h0n[:, t, (half * 8 + q) * 128:(half * 8 + q + 1) * 128] for q in range(8)]

                        def ev(pt, res, t=t, half=half):
                            P.op("dve", lambda g: g.tensor_copy(out=h0L[:, t, half * 8:(half + 1) * 8, :].rearrange("p a s -> p (a s)"), in_=pt[:, 0:128]), r=[res], w=["h0L"])
                        transposes(srcs, ev, ["h0n"], f32=True)
                P.barrier()
                P.flush()

            xt = [sb("xtA%d" % i, [128, D], F32, sa) for i in range(2)]
            xn1 = sb("xnA", [128, D], BF16, sa); xn = [xn1, xn1]
            xnT = sb("xnT", [128, 8, 1024], BF16, sa)
            u8 = sb("u8", [128, 32, 8, 16], BF16, sa)
            Fb = sb("Fb", [128, 32, 128], BF16, sa)
            PB = sb("PB", [128, 32, 2, 3, 16], F32, sa)
            Hst = sb("Hst", [128, 33, 2, 16], F32, sa)
            Hbf = sb("Hbf", [128, 2, 16, 128], BF16, sa)
            YF = sb("YF", [128, 32, 128], BF16, sa)
            y8 = sb("y8", [128, 8, 512], BF16, sa)
            ckvf = [sb("ckvf%d" % i, [128, KVR], F32, sa) for i in range(2)]
            krraw = sb("krraw", [128, 1, 64], F32, sa)
            krf = [sb("krf%d" % i, [128, 1, 64], F32, sa) for i in range(2)]
            kcat = sb("kcat", [128, 320], BF16, sa)
            s5fin = sb("s5fin", [128, 2, 16], F32, sa)
            fsm = sb("fsm", [128, 16, 2, 16], F32, sa)
            fsT = sb("fsT", [32, 128], F32, sa)

            P.op("pool", lambda g: g.memset(Hst[:, 0, :, :], 0.0), w=["Hst0"])
            P.op("pool", lambda g: g.memset(Hbf[:], 0.0), w=["Hbf"])

            supers = [list(range(0, 8)), list(range(8, 16)), list(range(16, 24)), list(range(24, 32)), [32, 33]]
            for si, tiles in enumerate(supers):
                ntok = 128 * len(tiles)
                nch = ntok // 8
                for li, ti in enumerate(tiles):
                    b = ti % 2
                    load("sp", xt[b][:], xall[ti * 128:(ti + 1) * 128, :], ["xtA%d" % b], "xA%d" % b)
                    rmsnorm_tile(xt[b][:], "xtA%d" % b, gmix[:], "gmix", xn[b][:], "xnA", col=0)

                    def ev(pt, res, li=li):
                        copy(evac_eng(), xnT[:, :, li * 128:(li + 1) * 128], pt[:, :].rearrange("p (a t) -> p a t", a=8), [res], ["xnT%d" % li])
                    transposes([xn[b][:, dt * 128:(dt + 1) * 128] for dt in range(8)], ev, ["xnA"])
                    pm, pres = mm_bank()
                    for dt in range(8):
                        P.op("pe", lambda g, dt=dt, li=li, pm=pm: g.matmul(pm[:, 0:320], lhsT=xnT[:, dt, li * 128:(li + 1) * 128], rhs=w_inA[:, dt, 512:832],
                                                                            start=(dt == 0), stop=(dt == 7)), r=["xnT%d" % li, "w_inA"], w=[pres])
                    cf = ckvf[b]; cres = "ckvf%d" % b
                    P.op("act", lambda g, pm=pm: g.activation(out=scr[:, 0:256], in_=pm[:, 0:256], func=AF.Square, accum_out=stat[:, 1:2]), r=[pres], w=["scr", "st1"])
                    rstd_from_ss(1, KVR)
                    P.op("dve", lambda g, pm=pm, cf=cf: g.scalar_tensor_tensor(out=cf[:], in0=pm[:, 0:256], scalar=stat[:, 1:2], in1=gkv[:], op0=ALU.mult, op1=ALU.mult),
                         r=[pres, "st1", "gkv"], w=[cres])
                    P.op("act", lambda g, pm=pm: g.copy(out=krraw[:, 0, :], in_=pm[:, 256:320]), r=[pres], w=["krraw"])
                    kf_ = krf[b]; kres = "krf%d" % b
                    load("sp", ropeA[b][:], ropeall[ti * 128:(ti + 1) * 128, :], ["ropeA%d" % b], "xA%d" % b)
                    rope_tile(krraw[:], ["krraw"], ropeA[b][:, 0:32], ropeA[b][:, 32:64], "ropeA%d" % b, kf_[:], kres, 1)
                    load("pool", o_ckv[ti * 128:(ti + 1) * 128, :], cf[:], [], "oA", r=[cres])
                    load("pool", o_kr[ti * 128:(ti + 1) * 128, :], kf_[:, 0, :], [], "oA", r=[kres + "a", kres + "b"])
                    P.op("pool", lambda g, cf=cf, ti=ti: g.tensor_copy(out=V[:, ti, :], in_=cf[:]), r=[cres], w=["V%d" % ti])
                    P.op("pool", lambda g, cf=cf: g.tensor_copy(out=kcat[:, 0:256], in_=cf[:]), r=[cres], w=["kcat"])
                    P.op("pool", lambda g, kf_=kf_: g.tensor_copy(out=kcat[:, 256:320], in_=kf_[:, 0, :]), r=[kres + "a", kres + "b"], w=["kcat"])

                    def ev(pt, res, ti=ti):
                        e = evac_eng()
                        copy(e, KT[:, 0:2, ti * 128:(ti + 1) * 128], pt[:, 0:256].rearrange("p (a t) -> p a t", a=2), [res], ["KT%d" % ti])
                        copy(e, KT[0:64, 2, ti * 128:(ti + 1) * 128], pt[0:64, 256:384], [res], ["KT%d" % ti])
                    transposes([kcat[:, 0:128], kcat[:, 128:256], kcat[:, 256:320]], ev, ["kcat"])
                xres = ["xnT%d" % li for li in range(len(tiles))]
                for j in range(8):
                    pm, pres = mm_bank()
                    lhs_all = xnT[:, :, 0:ntok].rearrange("p a (c j) -> p a j c", j=8)
                    for dt in range(8):
                        lhs = lhs_all[:, dt, j, :]
                        P.op("pe", lambda g, dt=dt, lhs=lhs, pm=pm: g.matmul(pm[0:nch, :], lhsT=lhs, rhs=w_inA[:, dt, 0:512], start=(dt == 0), stop=(dt == 7)),
                             r=xres + ["w_inA"], w=[pres])
                    copy(evac_eng(), u8[0:nch, :, j, :], pm[0:nch, :].rearrange("p (g h) -> p g h", h=16), [pres], ["u8_%d" % j])
                u8res = ["u8_%d" % j for j in range(8)]
                for g8 in range(4):
                    pt, res = tr_bank()
                    for q in range(8):
                        g_ = g8 * 8 + q
                        P.op("pe", lambda g, g_=g_, q=q, pt=pt: g.transpose(out=pt[:, q * 128:q * 128 + nch], in_=u8[0:nch, g_, :, :].rearrange("p j h -> p (j h)"),
                                                                             identity=ident[0:nch, 0:nch]), r=u8res + ["ident"], w=[res])
                    copy(evac_eng(), Fb[:, g8 * 8:(g8 + 1) * 8, 0:nch], pt[:, :].rearrange("p (g c) -> p g c", g=8)[:, :, 0:nch], [res], ["Fb%d" % g8])
                Fres = ["Fb%d" % g8 for g8 in range(4)]
                nsb = (nch + 31) // 32
                for sbi in range(nsb):
                    c0 = sbi * 32
                    ncs = min(32, nch - c0)
                    for g_ in range(32):
                        lo = 64 * (g_ % 2)
                        for o in range(2):
                            pq = ps_qk[o]
                            P.op("pe", lambda g, g_=g_, o=o, lo=lo, pq=pq: g.matmul(pq[lo:lo + 64, (g_ // 2) * 32:(g_ // 2) * 32 + ncs], lhsT=Wg[:, g_, o * 64:(o + 1) * 64],
                                                                                    rhs=Fb[:, g_, c0:c0 + ncs], start=True, stop=True), r=Fres + ["Wg"], w=["ps_qk%d" % o])
                    for o in range(2):
                        pq = ps_qk[o]
                        copy("act", PB[:, 0:ncs, o, 2, :], pq[:, :].rearrange("p (g c) -> p c g", c=32)[:, 0:ncs, :], ["ps_qk%d" % o], ["PBs%d" % o])
                    for i in range(ncs):
                        samp = (si == 4 and c0 + i >= 16)
                        if samp:
                            seq = c0 + i - 16
                            xprev = h0L[:, :, :, seq].unsqueeze(1).to_broadcast([128, 2, 2, 16])
                            xr = ["h0L"]
                        else:
                            xprev = Hst[:, i:i + 1, :, :].to_broadcast([128, 2, 2, 16])
                            xr = ["Hst%d" % i]
                            if si == 4 and c0 + i >= 2:
                                P.op("dve", lambda g, i=i: g.tensor_copy(out=Hst[:, i + 1, :, :], in_=Hst[:, i, :, :]), r=xr, w=["Hst%d" % (i + 1)])
                                continue
                        P.op("dve", lambda g, i=i, xprev=xprev: g.tensor_tensor(out=PB[:, i, :, 0:2, :], in0=xprev, in1=Kc[:], op=ALU.mult),
                             r=xr + ["Kc0", "Kc1"], w=["PBx%d" % i])
                        if samp:
                            outp = fsm[:, seq, :, :]; wres = "fsm"
                        else:
                            outp = Hst[:, i + 1, :, :]; wres = "Hst%d" % (i + 1)
                        P.op("dve", lambda g, i=i, outp=outp: g.tensor_reduce(out=outp, in_=PB[:, i, :, :, :].rearrange("p o t g -> p o g t"), axis=AX.X, op=ALU.add),
                             r=["PBx%d" % i, "PBs0", "PBs1"], w=[wres])
                    if si == 4:
                        P.op("pool", lambda g: g.tensor_copy(out=Hbf[:, :, :, 0:16], in_=Hst[:, 0:16, :, :].rearrange("p c t g -> p t g c")),
                             r=["Hst%d" % i for i in range(16)], w=["Hbf"])
                        P.op("pool", lambda g: g.tensor_copy(out=Hbf[:, :, :, 16:32], in_=h0L[:]), r=["h0L"], w=["Hbf"])
                        P.op("pool", lambda g: g.tensor_copy(out=s5fin[:, :, :], in_=Hst[:, 2, :, :]), r=["Hst2"], w=["s5fin"])
                    else:
                        P.op("pool", lambda g, c0=c0: g.tensor_copy(out=Hbf[:, :, :, c0:c0 + 32], in_=Hst[:, 0:32, :, :].rearrange("p c t g -> p t g c")),
                             r=["Hst%d" % i for i in range(32)], w=["Hbf"])
                        P.op("dve", lambda g: g.tensor_copy(out=Hst[:, 0, :, :], in_=Hst[:, 32, :, :]), r=["Hst%d" % i for i in range(33)], w=["Hst0"])
                for g4 in range(8):
                    pm, pres = mm_bank()
                    for q in range(4):
                        g_ = g4 * 4 + q
                        lo = 64 * (g_ % 2)
                        o_ = pm[:, q * 128:q * 128 + nch]
                        P.op("pe", lambda g, g_=g_, o_=o_: g.matmul(o_, lhsT=Tg[:, g_, :], rhs=Fb[:, g_, 0:nch], start=True, stop=False), r=Fres + ["Tg"], w=[pres])
                        for t in range(2):
                            P.op("pe", lambda g, g_=g_, o_=o_, t=t, lo=lo: g.matmul(o_, lhsT=Cw[:, t, g_, :], rhs=Hbf[:, t, g_ // 2, 0:nch],
                                                                                     start=False, stop=(t == 1)), r=["Hbf", "Cw0", "Cw1"], w=[pres])
                    copy(evac_eng(), YF[:, g4 * 4:(g4 + 1) * 4, 0:nch], pm[:, :].rearrange("p (g c) -> p g c", g=4)[:, :, 0:nch], [pres], ["YF%d" % g4])
                YFres = ["YF%d" % g4 for g4 in range(8)]
                for g8 in range(4):
                    pt, res = tr_bank()
                    for q in range(8):
                        g_ = g8 * 8 + q
                        P.op("pe", lambda g, g_=g_, q=q, pt=pt: g.transpose(out=pt[0:nch, q * 128:(q + 1) * 128], in_=YF[:, g_, 0:nch], identity=ident[:]),
                             r=YFres + ["ident"], w=[res])
                    copy(evac_eng(), y8[0:nch, :, g8 * 128:(g8 + 1) * 128].rearrange("p j (g h) -> p g j h", h=16),
                         pt[0:nch, :].rearrange("p (g j h) -> p g j h", g=8, j=8), [res], ["y8_%d" % g8])
                y8res = ["y8_%d" % g8 for g8 in range(4)]
                yv = y8[0:nch].rearrange("p j c -> p (j c)"); a1 = u8[0:nch].rearrange("p g j h -> p (g j h)")
                tt(a1, yv, yv, ALU.mult, y8res, u8res + ["ge1"])
                P.op("dve", lambda g: g.tensor_scalar(out=a1, in0=a1, scalar1=0.044715, scalar2=1.0, op0=ALU.mult, op1=ALU.add), r=["ge1"], w=["ge1"])
                tt(a1, a1, yv, ALU.mult, ["ge1"] + y8res, ["ge1"])
                P.op("act", lambda g: g.activation(out=a1, in_=a1, func=AF.Sigmoid, scale=1.5957691216), r=["ge1"], w=["ge1"])
                tt(a1, a1, yv, ALU.mult, ["ge1"] + y8res, ["ge1"])
                tok0 = tiles[0] * 128
                load("pool", ys_d[tok0:tok0 + ntok, :].rearrange("(c j) f -> c (j f)", j=8), a1, ["ys_d"], "oA", r=["ge1"] + u8res)

            def ev(pt, res):
                P.op("dve", lambda g: g.tensor_copy(out=fsT[:, :], in_=pt[0:32, 0:128]), r=[res], w=["fsT"])
            transposes([s5fin[:, :, :].rearrange("p o g -> p (o g)")], ev, ["s5fin"], f32=True)
            load("pool", o_s5p[:, :], fsT[:, :], [], "oA", r=["fsT"])
            for o in range(2):
                for hf in range(2):
                    for q2 in range(2):
                        q4 = hf * 2 + q2

                        def ev(pt, res, q2=q2):
                            P.op("dve", lambda g: g.tensor_copy(out=scr[0:16, q2 * 512:(q2 + 1) * 512], in_=pt[0:16, 0:512]), r=[res], w=["scr"])
                        transposes([fsm[:, :, o, q4 * 4 + q] for q in range(4)], ev, ["fsm"], f32=True)
                    load("pool", o_s5s[o * 16:(o + 1) * 16, hf * 1024:(hf + 1) * 1024], scr[0:16, :], [], "oA", r=["scr"])
            P.barrier()
            P.flush()
        if _STOP_AFTER == "A":
            P.flush(final=True)
            return nc

        with ExitStack() as sw:
            wstage = [sb("wstage%d" % i, [128, DFF], F32, sw) for i in range(2)]
            wtile = sb("wtile", [128, NFT, 1024], BF16, sw)
            wtileD = sb("wtileD", [128, NFT, 1024], BF16, sw)
            ci = 0
            for wsrc, wdst, nm in ((w_gate, wg_t, "g"), (w_up, wu_t, "u")):
                for kt in range(8):
                    b = ci % 2; ci += 1
                    load("sp", wstage[b][:], wsrc[kt * 128:(kt + 1) * 128, :], ["wstage%d" % b], "wst%d" % b)
                    e = ("pool", "dve", "act")[ci % 3]
                    copy(e, wtile[:, :, kt * 128:(kt + 1) * 128], wstage[b][:].rearrange("p (f n) -> p f n", n=128), ["wstage%d" % b], ["wtile%d" % kt])
                for f in range(NFT):
                    load("sp", wdst[f], wtile[:, f, :], ["w" + nm + "_t"], "wsto", r=["wtile%d" % kt for kt in range(8)])
            for f in range(NFT):
                b = ci % 2; ci += 1
                load("sp", wstage[b][:, 0:1024], w_down[f * 128:(f + 1) * 128, :], ["wstage%d" % b], "wst%d" % b)
                e = ("pool", "dve", "act")[ci % 3]
                copy(e, wtileD[:, f, :], wstage[b][:, 0:1024], ["wstage%d" % b], ["wtd%d" % f])
            for f in range(NFT):
                load("sp", wd_t[f], wtileD[:, f, :], ["wd_t"], "wsto", r=["wtd%d" % f])
            P.barrier()
            P.flush()
        if _STOP_AFTER == "W":
            P.flush(final=True)
            return nc

        with ExitStack() as sq_:
            w_inq = sb("w_inq", [128, 8, QR], BF16, sq_)
            wuq = sb("wuq", [128, 3, 768], BF16, sq_)
            wuk_n = sb("wuk_n", [128, 2, 512], BF16, sq_)
            wukT = sb("wukT", [128, 4, 256], BF16, sq_)
            wuv = sb("wuv", [128, 2, 512], BF16, sq_)
            wglu = sb("wglu", [128, 4, 512], BF16, sq_)
            wout = sb("wout", [128, 8, D], BF16, sq_)
            for kt in range(8):
                load("pool", w_inq[:, kt, :], w_in[kt * 128:(kt + 1) * 128, 512:896], ["w_inq"], "wB")
                load("pool", wout[:, kt, :], w_out[kt * 128:(kt + 1) * 128, :], ["wout"], "wB")
            for kt in range(3):
                load("pool", wuq[:, kt, :], w_uq[kt * 128:(kt + 1) * 128, :], ["wuq"], "wB")
            for kt in range(2):
                load("pool", wuk_n[:, kt, :], w_uk[kt * 128:(kt + 1) * 128, :], ["wuk_n"], "wB")
                load("pool", wuv[:, kt, :], w_uv[kt * 128:(kt + 1) * 128, :], ["wuv"], "wB")
            for kt in range(4):
                load("pool", wglu[:, kt, :], w_glu[kt * 128:(kt + 1) * 128, :], ["wglu"], "wB")
            for h in range(4):
                def ev(pt, res, h=h):
                    copy(evac_eng(), wukT[:, h, :], pt[:, 0:256], [res], ["wukT"])
                transposes([wuk_n[:, rt, h * 128:(h + 1) * 128] for rt in range(2)], ev, ["wuk_n"])
            ropeO = sb("ropeO", [128, 18, 64], F32, sq_)
            load("sp", ropeO[:, 0:16, :], ropeown[0:2048, :].rearrange("(n p) f -> p n f", p=128), ["ropeO"], "c1")
            load("sp", ropeO[0:64, 16, :], ropeown[2048:2112, :], ["ropeO"], "c1")
            load("sp", ropeO[:, 17, :], ropeown[2112:2240, :], ["ropeO"], "c1")

            if _DBG.get("sstop") == 1:
                P.barrier(); P.flush(final=True); return nc
            xq = sb("xq", [128, D], F32, sq_)
            xnq = sb("xnq", [128, D], BF16, sq_)
            xnqT = sb("xnqT", [128, 8, 128], BF16, sq_)
            cqn = sb("cqn", [128, QR], BF16, sq_)
            cqT = sb("cqT", [128, 3, 128], BF16, sq_)
            qn = sb("qn", [128, 4, 128], BF16, sq_)
            qrr = sb("qrr", [128, 4, 64], F32, sq_)
            qrb = sb("qrb", [128, 4, 64], BF16, sq_)
            qnT = sb("qnT", [128, 4, 128], BF16, sq_)
            qrT = sb("qrT", [64, 4, 128], BF16, sq_)
            qlT = sb("qlT", [128, 4, 2, 128], BF16, sq_)
            Pb = [sb("Pb%d" % i, [128, 512], BF16, sq_) for i in range(4)]
            PT = [sb("PT%d" % i, [128, 4, 128], BF16, sq_) for i in range(4)]
            Osb = sb("Osb", [128, 4, 256], F32, sq_)
            mst = sb("mst", [128, 4, 8], F32, sq_)
            P.op("pool", lambda g: g.memset(xq[:], 0.0), w=["xq"])

            def q_path(xsrc_ap, ropecs, mrows=128):
                load("sp", xq[0:mrows, :], xsrc_ap, ["xq"], "xq")
                rmsnorm_tile(xq[:], "xq", gmix[:], "gmix", xnq[:], "xnq", col=2)

                def ev(pt, res):
                    copy(evac_eng(), xnqT[:], pt[:, :].rearrange("p (a t) -> p a t", a=8), [res], ["xnqT"])
                transposes([xnq[:, dt * 128:(dt + 1) * 128] for dt in range(8)], ev, ["xnq"])
                pm, pres = mm_bank()
                for dt in range(8):
                    P.op("pe", lambda g, dt=dt, pm=pm: g.matmul(pm[:, 0:QR], lhsT=xnqT[:, dt, :], rhs=w_inq[:, dt, :], start=(dt == 0), stop=(dt == 7)),
                         r=["xnqT", "w_inq"], w=[pres])
                rmsnorm_tile(pm[:, 0:QR], pres, gq[:], "gq", cqn[:], "cqn", n=QR, col=3)

                def ev(pt, res):
                    copy(evac_eng(), cqT[:], pt[:, 0:384].rearrange("p (a t) -> p a t", a=3), [res], ["cqT"])
                transposes([cqn[:, k * 128:(k + 1) * 128] for k in range(3)], ev, ["cqn"])
                for half in range(2):
                    pm, pres = mm_bank()
                    for k in range(3):
                        P.op("pe", lambda g, k=k, pm=pm, half=half: g.matmul(pm[:, 0:384], lhsT=cqT[:, k, :], rhs=wuq[:, k, half * 384:(half + 1) * 384],
                                                                             start=(k == 0), stop=(k == 2)), r=["cqT", "wuq"], w=[pres])
                    pv = pm[:, 0:384].rearrange("p (h c) -> p h c", h=2)
                    copy("act", qn[:, half * 2:half * 2 + 2, :], pv[:, :, 0:128], [pres], ["qn%d" % half])
                    copy("act", qrr[:, half * 2:half * 2 + 2, :], pv[:, :, 128:192], [pres], ["qrr%d" % half])
                rope_tile(qrr[:], ["qrr0", "qrr1"], ropecs[:, 0:32], ropecs[:, 32:64], "ropeO", qrb[:], "qrb", 4, eng="dve")

                def ev(pt, res):
                    copy(evac_eng(), qnT[:], pt[:, 0:512].rearrange("p (h t) -> p h t", h=4), [res], ["qnT"])
                transposes([qn[:, h, :] for h in range(4)], ev, ["qn0", "qn1"])

                def ev(pt, res):
                    P.op("dve", lambda g: g.tensor_scalar(out=qrT[:], in0=pt[0:64, 0:512].rearrange("p (h t) -> p h t", h=4), scalar1=SCALE, scalar2=None, op0=ALU.mult), r=[res], w=["qrT"])
                transposes([qrb[:, h, :] for h in range(4)], ev, ["qrba", "qrbb"])
                for hp in range(2):
                    pm, pres = mm_bank()
                    for hh in range(2):
                        h = hp * 2 + hh
                        for rt in range(2):
                            P.op("pe", lambda g, h=h, hh=hh, rt=rt, pm=pm: g.matmul(pm[:, (hh * 2 + rt) * 128:(hh * 2 + rt + 1) * 128], lhsT=wukT[:, h, rt * 128:(rt + 1) * 128],
                                                                                    rhs=qnT[:, h, :], start=True, stop=True), r=["wukT", "qnT"], w=[pres])
                    P.op("dve", lambda g, pm=pm, hp=hp: g.tensor_scalar(out=qlT[:, hp * 2:hp * 2 + 2, :, :].rearrange("p h r t -> p (h r t)"), in0=pm[:, :], scalar1=SCALE, scalar2=None, op0=ALU.mult),
                         r=[pres], w=["qlT%d" % hp])

            def softmax_chunk(pq, pqres, rows, ncols, first, pb, pbres, hi=0):
                S_ = lambda k: mst[0:rows, hi, k:k + 1]
                m = S_(0); l = S_(1); mc = S_(2); mn = S_(3); al = S_(4); ng = S_(5); rs = S_(6)
                N = lambda x: "%s_%d" % (x, hi)
                P.op("dve", lambda g: g.reduce_max(out=mc, in_=pq[0:rows, 0:ncols], axis=AX.X), r=[pqres], w=[N("mc")])
                if first:
                    P.op("dve", lambda g: g.tensor_copy(out=m, in_=mc), r=[N("mc")], w=[N("m")])
                else:
                    tt(mn, m, mc, ALU.max, [N("m"), N("mc")], [N("mn")])
                    tt(al, m, mn, ALU.subtract, [N("m"), N("mn")], [N("al")])
                    P.op("act", lambda g: g.activation(out=al, in_=al, func=AF.Exp), r=[N("al")], w=[N("al")])
                    P.op("dve", lambda g: g.tensor_copy(out=m, in_=mn), r=[N("mn"), N("al")], w=[N("m")])
                P.op("dve", lambda g: g.tensor_scalar(out=ng, in0=m, scalar1=-1.0, scalar2=None, op0=ALU.mult), r=[N("m")], w=[N("ng")])
                P.op("act", lambda g: g.activation(out=pb[0:rows, 0:ncols], in_=pq[0:rows, 0:ncols], func=AF.Exp, bias=ng, accum_out=rs), r=[pqres, N("ng")], w=[pbres, N("rs")])
                if first:
                    P.op("dve", lambda g: g.tensor_copy(out=l, in_=rs), r=[N("rs")], w=[N("l")])
                else:
                    P.op("dve", lambda g: g.scalar_tensor_tensor(out=l, in0=l, scalar=al, in1=rs, op0=ALU.mult, op1=ALU.add), r=[N("l"), N("al"), N("rs")], w=[N("l")])

            def o_update(pvv, pvres, osl, ores, rows, first, hi=0):
                if first:
                    P.op("dve", lambda g: g.tensor_copy(out=osl, in_=pvv), r=[pvres], w=[ores])
                else:
                    P.op("dve", lambda g: g.scalar_tensor_tensor(out=osl, in0=osl, scalar=mst[0:rows, hi, 4:5], in1=pvv, op0=ALU.mult, op1=ALU.add), r=[pvres, ores, "al_%d" % hi], w=[ores])

            def pv_bank(rows):
                i = nxt("pv")
                return ps_pv[0:rows, i * 256:(i + 1) * 256], "ps_pv%d" % i

            with ExitStack() as ss_:
                NPG = 4
                mSM = sb("mSM", [128, 4, 128], BF16, ss_)
                load("sp", mSM[:].rearrange("p a k -> p (a k)"), masksm, ["mSM"], "c1")
                ptsel = sb("ptsel", [128, 256], I32, ss_)
                idxp = sb("idxp", [128, 256], I32, ss_)
                tbc = sb("tbc", [128, 1], F32, ss_)
                load("sp", tbc[:], tbcol, ["tbc"], "c1")
                for j in range(4):
                    load("sp", ptsel[j * 32:(j + 1) * 32, :], ptab[0:1, :].rearrange("a (c j) -> a j c", j=4)[:, j, :].partition_broadcast(32),
                         ["ptsel"], "c1", slow=True)
                P.op("dve", lambda g: g.tensor_scalar(out=idxp[:], in0=ptsel[:], scalar1=32.0, scalar2=tbc[:, 0:1], op0=ALU.mult, op1=ALU.add),
                     r=["ptsel", "tbc"], w=["idxp"])
                qcs = sb("qcs", [128, 3, 16, 32], BF16, ss_)
                pgk = [sb("pgk%d" % i, [128, 4, 4, KVR], BF16, ss_) for i in range(2)]
                pgr = [sb("pgr%d" % i, [128, 4, 4, ROPE], BF16, ss_) for i in range(2)]
                KTp = sb("KTp", [128, 3, 4 * NPG, 128], BF16, ss_)
                osm = sb("osm", [128, 4, 256], BF16, ss_)
                ckv_blk = cckv.rearrange("n (tb f) -> (n tb) f", tb=32)
                ckr_blk = ckr.rearrange("n (tb f) -> (n tb) f", tb=32)

                if _DBG.get("sstop") == 2:
                    P.barrier(); P.flush(final=True); return nc
                q_path(xall[33 * 128:34 * 128, :], ropeO[:, 17, :])
                if _DBG.get("sstop") == 3:
                    P.barrier(); P.flush(final=True); return nc
                for h in range(4):
                    for rt in range(2):
                        P.op("pool", lambda g, h=h, rt=rt: g.tensor_copy(out=qcs[:, rt, :, h * 8:(h + 1) * 8], in_=qlT[:, h, rt, :].rearrange("p (s q) -> p s q", q=8)),
                             r=["qlT0", "qlT1"], w=["qcs"])
                    P.op("pool", lambda g, h=h: g.tensor_copy(out=qcs[0:64, 2, :, h * 8:(h + 1) * 8], in_=qrT[:, h, :].rearrange("p (s q) -> p s q", q=8)), r=["qrT"], w=["qcs"])
                for pk in range(_DBG.get("npk", 4)):
                    nsteps = _DBG.get("nsteps", NPAGES // NPG) + 1
                    for st_ in range(nsteps):
                        first = (st_ == 0)
                        lastnew = (st_ == nsteps - 1)
                        if lastnew and _DBG.get("nolast"):
                            continue
                        iq = nxt("qk"); pq, pqres = ps_qk[iq], "ps_qk%d" % iq
                        if not lastnew:
                            grp, h2 = st_, 0
                            pb_i = grp % 2
                            kbuf, rbuf = pgk[pb_i], pgr[pb_i]
                            kres, rres = "pgk%d" % pb_i, "pgr%d" % pb_i
                            if h2 == 0:
                                for sq in range(4):
                                    c = (pk * 4 + sq) * 16 + grp
                                    P.dma("pool", lambda g, sq=sq, c=c, kbuf=kbuf: g.indirect_dma_start(
                                        out=kbuf[:, sq, :, :].rearrange("p t f -> p (t f)"), out_offset=None, in_=ckv_blk,
                                        in_offset=bass.IndirectOffsetOnAxis(ap=idxp[:, c:c + 1], axis=0)), r=["idxp"], w=[kres], key="pg%d" % pb_i)
                                    P.dma("pool", lambda g, sq=sq, c=c, rbuf=rbuf: g.indirect_dma_start(
                                        out=rbuf[:, sq, :, :].rearrange("p t f -> p (t f)"), out_offset=None, in_=ckr_blk,
                                        in_offset=bass.IndirectOffsetOnAxis(ap=idxp[:, c:c + 1], axis=0)), r=["idxp"], w=[rres], key="pg%d" % pb_i)
                            for sq in range(4):
                                pt, res = tr_bank()
                                for tl in range(NPG):
                                    for a_ in range(2):
                                        P.op("pe", lambda g, sq=sq, tl=tl, a_=a_, pt=pt, kbuf=kbuf: g.transpose(
                                            out=pt[:, (tl * 2 + a_) * 128:(tl * 2 + a_ + 1) * 128], in_=kbuf[:, sq, h2 * NPG + tl, a_ * 128:(a_ + 1) * 128], identity=ident[:]),
                                            r=[kres, "ident"], w=[res])
                                copy(evac_eng(), KTp[:, 0:2, sq * NPG:(sq + 1) * NPG, :], pt[:, :].rearrange("p (t a k) -> p a t k", t=NPG, a=2), [res], ["KTpa%d" % sq])
                            for sq2 in range(2):
                                pt, res = tr_bank()
                                for s2 in range(2):
                                    sq = sq2 * 2 + s2
                                    for tl in range(NPG):
                                        P.op("pe", lambda g, sq=sq, s2=s2, tl=tl, pt=pt, rbuf=rbuf: g.transpose(
                                            out=pt[0:64, (s2 * NPG + tl) * 128:(s2 * NPG + tl + 1) * 128], in_=rbuf[:, sq, h2 * NPG + tl, :], identity=ident[:]),
                                            r=[rres, "ident"], w=[res])
                                copy(evac_eng(), KTp[0:64, 2, sq2 * 2 * NPG:(sq2 * 2 + 2) * NPG, :], pt[0:64, :].rearrange("p (t k) -> p t k", k=128), [res], ["KTpb%d" % sq2])
                            ktres = ["KTpa%d" % i for i in range(4)] + ["KTpb0", "KTpb1"]
                            ncols = NPG * 128
                            for sq in (0, 1, 3, 2):
                                seq = pk * 4 + sq
                                for kk in range(3):
                                    kp = 64 if kk == 2 else 128
                                    if sq == 3:
                                        o_ = pq[64:128, 0:ncols]
                                        l_ = qcs[0:kp, kk, seq - 1:seq + 1, :].rearrange("p s q -> p (s q)")
                                    else:
                                        o_ = pq[sq * 32:(sq + 1) * 32, 0:ncols]
                                        l_ = qcs[0:kp, kk, seq, :]
                                    P.op("pe", lambda g, sq=sq, kk=kk, kp=kp, o_=o_, l_=l_: g.matmul(o_, lhsT=l_, rhs=KTp[0:kp, kk, sq * NPG:(sq + 1) * NPG, :], start=(kk == 0), stop=(kk == 2)),
                                         r=["qcs"] + ktres, w=[pqres])
                        else:
                            ncols = 128
                            for kk in range(3):
                                kp = 64 if kk == 2 else 128
                                P.op("pe", lambda g, kk=kk, kp=kp, pq=pq: g.matmul(pq[:, 0:128], lhsT=qcs[0:kp, kk, pk * 4:(pk + 1) * 4, :].rearrange("p s q -> p (s q)"),
                                                                                  rhs=KT[0:kp, kk, 33 * 128:34 * 128], start=(kk == 0), stop=False), r=["qcs", "KT"], w=[pqres])
                            P.op("pe", lambda g, pq=pq, pk=pk: g.matmul(pq[:, 0:128], lhsT=ident[:], rhs=mSM[:, pk, :], start=False, stop=True), r=["ident", "mSM"], w=[pqres])
                        ib = nxt("pb", 4); pb, pbres = Pb[ib], "Pb%d" % ib
                        softmax_chunk(pq, pqres, 128, ncols, first, pb, pbres)
                        ntl = ncols // 128
                        ptt, ptres = tr_bank()
                        for t_ in range(ntl):
                            P.op("pe", lambda g, t_=t_, ptt=ptt, pb=pb: g.transpose(out=ptt[:, t_ * 128:(t_ + 1) * 128], in_=pb[:, t_ * 128:(t_ + 1) * 128], identity=ident[:]),
                                 r=[pbres, "ident"], w=[ptres])
                        ip = nxt("pt", 4); ptb, ptbres = PT[ip], "PT%d" % ip
                        copy(evac_eng(), ptb[:, 0:ntl, :], ptt[:, 0:ntl * 128].rearrange("p (a t) -> p a t", a=ntl), [ptres], [ptbres])
                        pvb, pvres = pv_bank(128)
                        if lastnew:
                            P.op("pe", lambda g, ptb=ptb, pvb=pvb: g.matmul(pvb[:, :], lhsT=ptb[:, 0, :], rhs=V[:, 33, :], start=True, stop=True), r=[ptbres, "V"], w=[pvres])
                        else:
                            for sq in (0, 1, 3, 2):
                                for t_ in range(ntl):
                                    if sq == 3:
                                        o_ = pvb[64:128, :]; l_ = ptb[:, t_, 64:128]
                                    else:
                                        o_ = pvb[sq * 32:(sq + 1) * 32, :]; l_ = ptb[:, t_, sq * 32:(sq + 1) * 32]
                                    P.op("pe", lambda g, sq=sq, t_=t_, o_=o_, l_=l_, kbuf=kbuf: g.matmul(o_, lhsT=l_, rhs=kbuf[:, sq, h2 * NPG + t_, :], start=(t_ == 0), stop=(t_ == ntl - 1)),
                                         r=[ptbres, kres], w=[pvres])
                        o_update(pvb, pvres, Osb[:, 0, :], "Osb0", 128, first)
                    P.op("dve", lambda g: g.reciprocal(out=mst[:, 0, 7:8], in_=mst[:, 0, 1:2]), r=["l_0"], w=["rl_0"])
                    P.op("dve", lambda g, pk=pk: g.tensor_scalar(out=osm[:, pk, :], in0=Osb[:, 0, :], scalar1=mst[:, 0, 7:8], scalar2=None, op0=ALU.mult), r=["Osb0", "rl_0"], w=["osm%d" % pk])
                for pk in range(4):
                    pt, res = tr_bank()
                    for rt in range(2):
                        P.op("pe", lambda g, rt=rt, pk=pk, pt=pt: g.transpose(out=pt[:, rt * 128:(rt + 1) * 128], in_=osm[:, pk, rt * 128:(rt + 1) * 128], identity=ident[:]),
                             r=["osm%d" % pk, "ident"], w=[res])
                    for rt in range(2):
                        copy(evac_eng(), oTs[:, :, rt, pk * 32:(pk + 1) * 32].rearrange("p h (s q) -> p s h q", q=8),
                             pt[:, rt * 128:(rt + 1) * 128].rearrange("p (s h q) -> p s h q", s=4, h=4), [res], ["oTs"])
                P.barrier()
                P.flush()

            if _STOP_AFTER == "S":
                P.flush(final=True)
                return nc
            with ExitStack() as sbb:
                mAB = sb("mAB", [128, 4, 128], BF16, sbb)
                mSP = sb("mSP", [64, TPAD], BF16, sbb)
                load("sp", mAB[:].rearrange("p a k -> p (a k)"), maskab, ["mAB"], "c1")
                load("sp", mSP[:], masksp, ["mSP"], "c1")
                hval = sb("hval", [128, 34], F32, sbb)
                load("sp", hval[:], halov, ["hval"], "c1")
                ixo = sb("ixo", [128, 17], I32, sbb)
                load("sp", ixo[:], idxown, ["ixo"], "c1")
                cw = sb("cw", [128, NFT, 4], F32, sbb)
                for k in range(3):
                    load("sp", cw[:, :, k], conv_w[k:k + 1, :].rearrange("a (f p) -> p (a f)", p=128), ["cw"], "c1", slow=True)
                load("sp", cw[:, :, 3], conv_b.rearrange("a (f p) -> p (a f)", p=128), ["cw"], "c1", slow=True)
                scT = sb("scT", [128, NFT, 32], F32, sbb)
                with ExitStack() as s3:
                    scn = sb("scn", [32, DFF], F32, s3)
                    load("sp", scn[:], stconv, ["scn"], "c1")
                    for f4 in range(6):
                        fl = list(range(f4 * 4, min(NFT, f4 * 4 + 4)))
                        for qi, f in enumerate(fl):
                            P.op("pe", lambda g, qi=qi, f=f: g.transpose(out=ps_trf[:, qi * 32:(qi + 1) * 32], in_=scn[:, f * 128:(f + 1) * 128], identity=identf[0:32, 0:32]),
                                 r=["scn", "identf"], w=["ps_trf"])
                        copy(evac_eng(), scT[:, fl[0]:fl[-1] + 1, :], ps_trf[:, 0:32 * len(fl)].rearrange("p (f s) -> p f s", s=32), ["ps_trf"], ["scT"])
                    P.barrier()
                    P.flush()

                obf = sb("obf", [128, 4, 256], BF16, sbb)
                oT = sb("oT", [128, 4, 2, 128], BF16, sbb)
                ymT = sb("ymT", [128, 4, 128], BF16, sbb)
                ysb = sb("ysb", [128, 512], BF16, sbb)
                yT = sb("yT", [128, 4, 128], BF16, sbb)
                sgT = sb("sgT", [128, 4, 128], BF16, sbb)
                ygT = sb("ygT", [128, 4, 128], BF16, sbb)
                x1 = sb("x1", [128, 2, D], F32, sbb)
                xn2 = xnq
                xn2T = sb("xn2T", [128, 8, 256], BF16, sbb)
                hT = sb("hT", [128, NFT, 256], BF16, sbb)
                gbuf = sb("gbuf", [128, 264], F32, sbb)
                cacc = sb("cacc", [128, 256], F32, sbb)
                sg = sb("sg", [128, 256], F32, sbb)
                hgate = sb("hgate", [128, NFT, 34], F32, sbb)
                wgs = [sb("wgs%d" % i, [128, 8, 128], BF16, sbb) for i in range(2)]
                wus = [sb("wus%d" % i, [128, 8, 128], BF16, sbb) for i in range(2)]
                wds = [sb("wds%d" % i, [128, D], BF16, sbb) for i in range(2)]
                x2 = sb("x2", [128, D], F32, sbb)
                yo = sb("yo", [128, D], F32, sbb)
                gts = sb("gts", [16, 2, 128], F32, sbb)
                gtp = sb("gtp", [2, 128], F32, sbb)
                P.op("pool", lambda g: g.memset(x1[:], 0.0), w=["x1_0", "x1_1"])
                P.op("pool", lambda g: g.memset(obf[:], 0.0), w=["obf0", "obf1", "obf2", "obf3"])
                P.op("pool", lambda g: g.memset(ysb[:], 0.0), w=["ysb"])
                P.op("pool", lambda g: g.memset(xn2T[:], 0.0), w=["xn2T"])

                def attention_prompt(rows, nkt, mask_fn):
                    nchunks = (nkt + 3) // 4
                    for ci_ in range(nchunks):
                        for h in range(4):
                            k0 = ci_ * 4
                            nt_ = min(4, nkt - k0)
                            ncols = nt_ * 128
                            pq, pqres = qk_banks[nxt("qk4", 4)]
                            cols = slice(k0 * 128, k0 * 128 + ncols)
                            masks = [(t_, mask_fn(k0 + t_)) for t_ in range(nt_)]
                            masks = [(t_, m_) for t_, m_ in masks if m_ is not None]
                            P.op("pe", lambda g, h=h, pq=pq, cols=cols: g.matmul(pq[0:rows, 0:ncols], lhsT=qlT[:, h, 0, 0:rows], rhs=KT[:, 0, cols], start=True, stop=False),
                                 r=["qlT0", "qlT1", "KT"], w=[pqres])
                            P.op("pe", lambda g, h=h, pq=pq, cols=cols: g.matmul(pq[0:rows, 0:ncols], lhsT=qlT[:, h, 1, 0:rows], rhs=KT[:, 1, cols], start=False, stop=False),
                                 r=["qlT0", "qlT1", "KT"], w=[pqres])
                            nm_ = len(masks)
                            P.op("pe", lambda g, h=h, pq=pq, cols=cols, nm_=nm_: g.matmul(pq[0:rows, 0:ncols], lhsT=qrT[0:64, h, 0:rows], rhs=KT[0:64, 2, cols], start=False, stop=(nm_ == 0)),
                                 r=["qrT", "KT"], w=[pqres])
                            for mi, (t_, m_) in enumerate(masks):
                                P.op("pe", lambda g, t_=t_, m_=m_, pq=pq, mi=mi, nm_=nm_: g.matmul(pq[0:rows, t_ * 128:(t_ + 1) * 128], lhsT=ident[0:rows, 0:rows], rhs=m_,
                                                                                                   start=False, stop=(mi == nm_ - 1)), r=["ident", "mAB", "mSP"], w=[pqres])
                            ib = nxt("pb", 4); pb, pbres = Pb[ib], "Pb%d" % ib
                            softmax_chunk(pq, pqres, rows, ncols, ci_ == 0, pb, pbres, hi=h)
                            ptt, ptres = tr_banks3[nxt("tr3", 3)]
                            for t_ in range(nt_):
                                P.op("pe", lambda g, t_=t_, ptt=ptt, pb=pb: g.transpose(out=ptt[:, t_ * 128:t_ * 128 + rows], in_=pb[0:rows, t_ * 128:(t_ + 1) * 128],
                                                                                        identity=ident[0:rows, 0:rows]), r=[pbres, "ident"], w=[ptres])
                            ip = nxt("pt", 4); ptb, ptbres = PT[ip], "PT%d" % ip
                            copy(evac_eng(), ptb[:, 0:nt_, 0:rows], ptt[:, 0:nt_ * 128].rearrange("p (a t) -> p a t", a=nt_)[:, :, 0:rows], [ptres], [ptbres])
                            pvv, pvres = pv_bank(rows)
                            for t_ in range(nt_):
                                P.op("pe", lambda g, t_=t_, ptb=ptb, pvv=pvv, k0=k0, nt_=nt_: g.matmul(pvv, lhsT=ptb[:, t_, 0:rows], rhs=V[:, k0 + t_, :], start=(t_ == 0), stop=(t_ == nt_ - 1)),
                                     r=[ptbres, "V"], w=[pvres])
                            o_update(pvv, pvres, Osb[0:rows, h, :], "Osb%d" % h, rows, ci_ == 0, hi=h)
                    for h in range(4):
                        P.op("dve", lambda g, h=h: g.reciprocal(out=mst[0:rows, h, 7:8], in_=mst[0:rows, h, 1:2]), r=["l_%d" % h], w=["rl_%d" % h])
                        P.op("dve", lambda g, h=h: g.tensor_scalar(out=obf[0:rows, h, :], in0=Osb[0:rows, h, :], scalar1=mst[0:rows, h, 7:8], scalar2=None, op0=ALU.mult),
                             r=["Osb%d" % h, "rl_%d" % h], w=["obf%d" % h])

                def mla_from(oT_, ores, rows):
                    pm, pres = mm_bank()
                    for h in range(4):
                        for rt in range(2):
                            P.op("pe", lambda g, h=h, rt=rt, pm=pm: g.matmul(pm[:, h * 128:h * 128 + rows], lhsT=wuv[:, rt, h * 128:(h + 1) * 128], rhs=oT_[:, h, rt, 0:rows],
                                                                              start=(rt == 0), stop=(rt == 1)), r=["wuv", ores], w=[pres])
                    copy(evac_eng(), ymT[:, :, 0:rows], pm[:, :].rearrange("p (h t) -> p h t", h=4)[:, :, 0:rows], [pres], ["ymT"])

                def obf_to_oT(rows):
                    pt, res = tr_bank()
                    for h in range(4):
                        for rt in range(2):
                            P.op("pe", lambda g, h=h, rt=rt, pt=pt: g.transpose(out=pt[:, (h * 2 + rt) * 128:(h * 2 + rt) * 128 + rows], in_=obf[0:rows, h, rt * 128:(rt + 1) * 128],
                                                                                identity=ident[0:rows, 0:rows]), r=["obf%d" % h, "ident"], w=[res])
                    copy(evac_eng(), oT[:, :, :, 0:rows], pt[:, :].rearrange("p (h r t) -> p h r t", h=4, r=2)[:, :, :, 0:rows], [res], ["oT"])

                def tail_common(rows, sig, col0):
                    pt, res = tr_bank()
                    for k in range(4):
                        P.op("pe", lambda g, k=k, pt=pt: g.transpose(out=pt[:, k * 128:k * 128 + rows], in_=ysb[0:rows, k * 128:(k + 1) * 128], identity=ident[0:rows, 0:rows]),
                             r=["ysb", "ident"], w=[res])
                    copy(evac_eng(), yT[:, :, 0:rows], pt[:, 0:512].rearrange("p (a t) -> p a t", a=4)[:, :, 0:rows], [res], ["yT"])
                    pm, pres = mm_bank()
                    for mt in range(4):
                        for k in range(4):
                            P.op("pe", lambda g, mt=mt, k=k, pm=pm: g.matmul(pm[:, mt * 128:mt * 128 + rows], lhsT=wglu[:, k, mt * 128:(mt + 1) * 128], rhs=yT[:, k, 0:rows],
                                                                              start=(k == 0), stop=(k == 3)), r=["wglu", "yT"], w=[pres])
                    P.op("act", lambda g, pm=pm: g.activation(out=sgT[:, :, 0:rows], in_=pm[:, :].rearrange("p (a t) -> p a t", a=4)[:, :, 0:rows], func=AF.Sigmoid), r=[pres], w=["sgT"])
                    tt(ygT[:, :, 0:rows], yT[:, :, 0:rows], sgT[:, :, 0:rows], ALU.mult, ["yT", "sgT"], ["ygT"])
                    for half in range(2):
                        pm, pres = mm_bank()
                        for k in range(8):
                            lhs = ygT[:, k, 0:rows] if k < 4 else ymT[:, k - 4, 0:rows]
                            P.op("pe", lambda g, k=k, lhs=lhs, pm=pm, half=half: g.matmul(pm[0:rows, :], lhsT=lhs, rhs=wout[:, k, half * 512:(half + 1) * 512],
                                                                                         start=(k == 0), stop=(k == 7)), r=["ygT", "ymT", "wout"], w=[pres])
                        tt(x1[0:rows, sig, half * 512:(half + 1) * 512], pm[0:rows, :], xq[0:rows, half * 512:(half + 1) * 512], ALU.add, [pres, "xq"], ["x1_%d" % sig])
                    rmsnorm_tile(x1[0:rows, sig, :], "x1_%d" % sig, gffn[0:rows, :], "gffn", xn2[0:rows, :], "xnq", col=4, rows=rows)
                    pt, res = tr_bank()
                    for dt in range(8):
                        P.op("pe", lambda g, dt=dt, pt=pt: g.transpose(out=pt[:, dt * 128:dt * 128 + rows], in_=xn2[0:rows, dt * 128:(dt + 1) * 128], identity=ident[0:rows, 0:rows]),
                             r=["xnq", "ident"], w=[res])
                    copy(evac_eng(), xn2T[:, :, col0:col0 + rows], pt[:, :].rearrange("p (a t) -> p a t", a=8)[:, :, 0:rows], [res], ["xn2T"])

                dacc = [(ps_qk[0], ["ps_qk0"]), (ps_qk[1], ["ps_qk1"]), (ps_pv, ["ps_pv0", "ps_pv1"]), (ps_trf, ["ps_trf"])]

                def ffn_group(ntok, tiles, out_fn, special, hidx=0):
                    for f in range(NFT):
                        b = f % 2
                        load("sp", wgs[b][:].rearrange("p a n -> p (a n)"), wg_t[f], ["wgs%d" % b], "wf%d" % b, r=["wg_t"])
                        load("sp", wus[b][:].rearrange("p a n -> p (a n)"), wu_t[f], ["wus%d" % b], "wf%d" % b, r=["wu_t"])
                        load("sp", wds[b][:], wd_t[f], ["wds%d" % b], "wf%d" % b, r=["wd_t"])
                        pg_, pu_ = ps_mm[0], ps_mm[1]
                        for dt in range(8):
                            P.op("pe", lambda g, dt=dt, b=b: g.matmul(pg_[:, 0:ntok], lhsT=wgs[b][:, dt, :], rhs=xn2T[:, dt, 0:ntok], start=(dt == 0), stop=(dt == 7)),
                                 r=["wgs%d" % b, "xn2T"], w=["ps_mm0"])
                        for dt in range(8):
                            P.op("pe", lambda g, dt=dt, b=b: g.matmul(pu_[:, 0:ntok], lhsT=wus[b][:, dt, :], rhs=xn2T[:, dt, 0:ntok], start=(dt == 0), stop=(dt == 7)),
                                 r=["wus%d" % b, "xn2T"], w=["ps_mm1"])
                        if special:
                            P.op("pool", lambda g: g.memset(gbuf[:, 0:2], 0.0), w=["gbufh"])
                            copy("act", gbuf[:, 2:66], pg_[:, 0:64], ["ps_mm0"], ["gbuf"])
                            gs = gbuf[:, 66:226].rearrange("p (s t) -> p s t", t=10)
                            copy("act", gs[:, :, 2:10], pg_[:, 64:192].rearrange("p (s t) -> p s t", t=8), ["ps_mm0"], ["gbuf2"])
                            P.op("pool", lambda g, f=f, gs=gs: g.tensor_copy(out=gs[:, :, 0:2], in_=scT[:, f, :].rearrange("p (s t) -> p s t", t=2)), r=["scT"], w=["gbufh2"])
                            tt(hgate[:, f, :], gbuf[:, 2:36], hval[:], ALU.mult, ["gbuf", "hval"], ["hgate"], e="pool")
                            for k_ in range(2):
                                P.op("pe", lambda g, gs=gs, k_=k_: g.transpose(out=ps_tr1f[0:16, k_ * 128:(k_ + 1) * 128], in_=gs[:, :, 8 + k_], identity=identf[:]), r=["gbuf2", "identf"], w=["ps_tr1"])
                            P.op("pe", lambda g: g.transpose(out=ps_tr1f[0:2, 256:384], in_=gbuf[:, 50:52], identity=identf[:]), r=["gbuf", "identf"], w=["ps_tr1"])
                            copy("dve", gts[:].rearrange("s k n -> s (k n)"), ps_tr1f[0:16, 0:256], ["ps_tr1"], ["gts"])
                            copy("dve", gtp[:], ps_tr1f[0:2, 256:384], ["ps_tr1"], ["gtp"])
                            load("pool", o_cvs.rearrange("(s k) n -> s k n", k=2)[:, :, f * 128:(f + 1) * 128], gts[:], [], "oB", r=["gts"])
                            load("pool", o_cvp[:, f * 128:(f + 1) * 128], gtp[:], [], "oB", r=["gtp"])
                            convs = [(gbuf[:, 2:66], gbuf[:, 1:65], gbuf[:, 0:64], cacc[:, 0:64]),
                                     (gs[:, :, 2:10], gs[:, :, 1:9], gs[:, :, 0:8], cacc[:, 64:192].rearrange("p (s t) -> p s t", t=8))]
                        else:
                            gv = gbuf[:, 0:260].rearrange("p (s t) -> p s t", t=130)
                            copy("act", gv[:, :, 2:130], pg_[:, 0:256].rearrange("p (s t) -> p s t", t=128), ["ps_mm0"], ["gbuf"])
                            P.op("pool", lambda g, f=f, gv=gv: g.tensor_copy(out=gv[:, :, 0:2], in_=hgate[:, f, hidx * 2:hidx * 2 + 4].rearrange("p (s t) -> p s t", t=2)),
                                 r=["hgate"], w=["gbufh"])
                            convs = [(gv[:, :, 2:130], gv[:, :, 1:129], gv[:, :, 0:128], cacc[:, 0:256].rearrange("p (s t) -> p s t", t=128))]
                        gres = ["gbuf", "gbufh", "gbuf2", "gbufh2", "cw"]
                        for (v2, v1, v0, acc) in convs:
                            P.op("dve", lambda g, v2=v2, acc=acc, f=f: g.tensor_scalar(out=acc, in0=v2, scalar1=cw[:, f, 2:3], scalar2=cw[:, f, 3:4], op0=ALU.mult, op1=ALU.add),
                                 r=gres, w=["cacc"])
                            P.op("dve", lambda g, v1=v1, acc=acc, f=f: g.scalar_tensor_tensor(out=acc, in0=v1, scalar=cw[:, f, 1:2], in1=acc, op0=ALU.mult, op1=ALU.add),
                                 r=gres + ["cacc"], w=["cacc"])
                            P.op("dve", lambda g, v0=v0, acc=acc, f=f: g.scalar_tensor_tensor(out=acc, in0=v0, scalar=cw[:, f, 0:1], in1=acc, op0=ALU.mult, op1=ALU.add),
                                 r=gres + ["cacc"], w=["cacc"])
                        P.op("act", lambda g: g.activation(out=sg[:, 0:ntok], in_=cacc[:, 0:ntok], func=AF.Silu), r=["cacc"], w=["sg"])
                        tt(hT[:, f, 0:ntok], sg[:, 0:ntok], pu_[:, 0:ntok], ALU.mult, ["sg", "ps_mm1"], ["hT%d" % f])
                        ai_ = 0
                        for (t0, tn, xi_) in tiles:
                            for half in range(2):
                                acc_, ar_ = dacc[ai_]; ai_ += 1
                                P.op("pe", lambda g, f=f, t0=t0, tn=tn, half=half, acc_=acc_, b=b: g.matmul(acc_[0:tn, :], lhsT=hT[:, f, t0:t0 + tn], rhs=wds[b][:, half * 512:(half + 1) * 512],
                                                                                                      start=(f == 0), stop=(f == NFT - 1)), r=["hT%d" % f, "wds%d" % b], w=ar_)
                    ai_ = 0
                    for ti_, (t0, tn, xi_) in enumerate(tiles):
                        for half in range(2):
                            acc_, ar_ = dacc[ai_]; ai_ += 1
                            tt(x2[0:tn, half * 512:(half + 1) * 512], acc_[0:tn, :], x1[0:tn, xi_, half * 512:(half + 1) * 512], ALU.add, ar_ + ["x1_%d" % xi_], ["x2"])
                        rmsnorm_tile(x2[0:tn, :], "x2", gfin[0:tn, :], "gfin", yo[0:tn, :], "yo", col=5, rows=tn)
                        out_fn(ti_, tn)

                q_path(xown[2048:2112, :], ropeO[:, 16, :], mrows=64)
                attention_prompt(64, 33, lambda kt: mSP[:, kt * 128:(kt + 1) * 128])
                obf_to_oT(64)
                mla_from(oT, "oT", 64)
                P.dma("pool", lambda g: g.indirect_dma_start(out=ysb[:, :], out_offset=None, in_=ys_d, in_offset=bass.IndirectOffsetOnAxis(ap=ixo[:, 16:17], axis=0)),
                      r=["ixo", "ys_d"], w=["ysb"], key="ysl")
                tail_common(64, 0, 0)
                load("sp", xq[:, :], xall[33 * 128:34 * 128, :], ["xq"], "xq")
                mla_from(oTs, "oTs", 128)
                load("sp", ysb[:], ys_d[33 * 128:34 * 128, :], ["ysb"], "ysl", r=["ys_d"])
                tail_common(128, 1, 64)

                def out_special(ti_, tn):
                    if ti_ == 0:
                        load("pool", o_y[2048:2112, :], yo[0:64, :], [], "oB", r=["yo"])
                    else:
                        load("pool", o_ys[:, :], yo[:, :], [], "oB", r=["yo"])
                ffn_group(192, [(0, 64, 0), (64, 128, 1)], out_special, True)

                for grp in range(NSLOT // 2):
                    for sl in range(2):
                        s = grp * 2 + sl
                        q_path(xown[s * 128:(s + 1) * 128, :], ropeO[:, s, :])
                        nkt = 2 * s + 2
                        sp_ = s % 2

                        def mask_fn(kt, nkt=nkt, sp_=sp_):
                            if kt == nkt - 2:
                                return mAB[:, sp_ * 2 + 0, :]
                            if kt == nkt - 1:
                                return mAB[:, sp_ * 2 + 1, :]
                            return None
                        attention_prompt(128, nkt, mask_fn)
                        obf_to_oT(128)
                        mla_from(oT, "oT", 128)

                        P.dma("pool", lambda g, s=s: g.indirect_dma_start(out=ysb[:, :], out_offset=None, in_=ys_d, in_offset=bass.IndirectOffsetOnAxis(ap=ixo[:, s:s + 1], axis=0)),
                              r=["ixo", "ys_d"], w=["ysb"], key="ysl")
                        tail_common(128, sl, sl * 128)

                    def out_reg(ti_, tn, grp=grp):
                        s = grp * 2 + ti_
                        load("pool", o_y[s * 128:(s + 1) * 128, :], yo[:, :], [], "oB", r=["yo"])
                    ffn_group(256, [(0, 128, 0), (128, 128, 1)], out_reg, False, hidx=grp * 2)
                P.barrier()
                P.flush()

        P.flush(final=True)
    return nc


def _rope_tab(pos):
    half = 32
    inv = (10000.0 ** (-np.arange(half, dtype=np.float32) / np.float32(half))).astype(np.float32)
    ang = pos.astype(np.float32)[:, None] * inv[None, :]
    return np.concatenate([np.cos(ang), np.sin(ang)], axis=1).astype(np.float32)


_NC_CACHE = {}


def prepare(x_prompt, x_sample, cache_ckv, cache_kr, state_s5_re, state_s5_im, state_conv, page_table,
           meta_tokens, g_mix, w_in, g_q, w_uq, g_kv, w_uk, w_uv, s5_a_re, s5_a_im, s5_log_dt,
           s5_b_re, s5_b_im, s5_c_re, s5_c_im, s5_d, w_glu, w_out, g_ffn, w_gate, w_up, conv_w, conv_b,
           w_down, g_final):
    f32 = np.float32
    A = lambda a: np.ascontiguousarray(np.asarray(a))
    x_prompt = A(x_prompt); x_sample = A(x_sample); meta_tokens = A(meta_tokens)
    cck = A(cache_ckv).reshape(NPOOL, 128 * KVR)
    ckr_ = A(cache_kr).reshape(NPOOL, 128 * ROPE)
    page_table = A(page_table).astype(np.int32)
    shared = {
        "cache_ckv": cck, "cache_kr": ckr_,
        "g_mix": A(g_mix).reshape(1, D), "w_in": A(w_in).reshape(D, INW), "g_q": A(g_q).reshape(1, QR),
        "w_uq": A(w_uq).reshape(QR, 768), "g_kv": A(g_kv).reshape(1, KVR),
        "w_uk": A(w_uk).reshape(KVR, 512), "w_uv": A(w_uv).reshape(KVR, 512),
        "s5_a_re": A(s5_a_re).reshape(32, 64), "s5_a_im": A(s5_a_im).reshape(32, 64), "s5_log_dt": A(s5_log_dt).reshape(1, 32),
        "s5_b_re": A(s5_b_re).reshape(32, 64, 16), "s5_b_im": A(s5_b_im).reshape(32, 64, 16),
        "s5_c_re": A(s5_c_re).reshape(512, 64), "s5_c_im": A(s5_c_im).reshape(512, 64),
        "s5_d": A(s5_d).reshape(1, 512), "w_glu": A(w_glu).reshape(512, 512), "w_out": A(w_out).reshape(D, D),
        "g_ffn": A(g_ffn).reshape(1, D), "w_gate": A(w_gate).reshape(D, DFF), "w_up": A(w_up).reshape(D, DFF),
        "conv_w": A(conv_w).reshape(3, DFF), "conv_b": A(conv_b).reshape(1, DFF), "w_down": A(w_down).reshape(DFF, D),
        "g_final": A(g_final).reshape(1, D),
        "identd": np.eye(128, dtype=f32),
    }
    ii = np.arange(128) // 16
    hh = np.arange(128) % 16
    shared["tmaskd"] = (ii[None, :] >= ii[:, None]).astype(f32)
    shared["ddiagd"] = ((ii[None, :] == ii[:, None]) & (hh[None, :] == hh[:, None])).astype(f32)
    msm = np.full((128, 4, 128), NEG, f32)
    for pk in range(4):
        for sq in range(4):
            for h in range(4):
                for q in range(8):
                    row = sq * 32 + h * 8 + q
                    seq = pk * 4 + sq
                    msm[row, pk, seq * 8:seq * 8 + q + 1] = 0.0
    shared["masksm"] = msm.reshape(128, 512).astype(NPBF)
    tri = np.where(np.arange(128)[None, :] <= np.arange(128)[:, None], 0.0, NEG).astype(f32)

    in_maps = []
    qbs = []
    for k in range(8):
        b, r = k // 2, k % 2
        hp = np.concatenate([meta_tokens, x_prompt[b]], axis=0)
        xs = x_sample[16 * k:16 * k + 16].reshape(128, D)
        xall = np.zeros((NTA * 128, D), f32)
        xall[:T] = hp
        xall[33 * 128:] = xs
        qb = [2 * s + ((s + r) % 2) for s in range(NSLOT)]
        qbs.append(qb)
        xown = np.zeros((NOWN, D), f32)
        pos_own = np.zeros(NOWN, np.int64)
        hv = np.ones(34, f32)
        idxown = np.zeros((128, 17), np.int32)
        for s in range(NSLOT):
            t0 = 128 * qb[s]
            xown[s * 128:(s + 1) * 128] = hp[t0:t0 + 128]
            pos_own[s * 128:(s + 1) * 128] = np.arange(t0, t0 + 128)
            idxown[:, s] = t0 + np.arange(128)
            if t0 >= 2:
                xown[2048 + 2 * s] = hp[t0 - 2]; xown[2048 + 2 * s + 1] = hp[t0 - 1]
                pos_own[2048 + 2 * s] = t0 - 2; pos_own[2048 + 2 * s + 1] = t0 - 1
                idxown[2 * s, 16] = t0 - 2; idxown[2 * s + 1, 16] = t0 - 1
            else:
                xown[2048 + 2 * s] = hp[0]; xown[2048 + 2 * s + 1] = hp[0]
                hv[2 * s] = 0.0; hv[2 * s + 1] = 0.0
        xown[2048 + 32] = hp[4094]; xown[2048 + 33] = hp[4095]
        pos_own[2048 + 32] = 4094; pos_own[2048 + 33] = 4095
        xown[2048 + 34:2048 + 50] = hp[4096:4112]
        idxown[32:50, 16] = np.arange(4094, 4112)
        pos_own[2048 + 34:2048 + 50] = np.arange(4096, 4112)
        xown[2048 + 50:] = hp[0]
        pos_s = 8192 + (np.arange(128) % 8)
        pos_all = np.concatenate([np.arange(33 * 128), pos_s])
        mab = np.zeros((128, 4, 128), f32)
        for sp_ in range(2):
            first = ((sp_ + r) % 2 == 0)
            if first:
                mab[:, sp_ * 2 + 0] = tri; mab[:, sp_ * 2 + 1] = NEG
            else:
                mab[:, sp_ * 2 + 0] = 0.0; mab[:, sp_ * 2 + 1] = tri
        kk = np.arange(TPAD)
        psp = pos_own[2048:2112]
        msp = np.where((kk[None, :] <= psp[:, None]) & (kk[None, :] < T), 0.0, NEG).astype(f32)
        m = dict(shared)
        m.update({
            "xall": xall, "xown": xown,
            "ropeall": _rope_tab(pos_all), "ropeown": _rope_tab(np.concatenate([pos_own, pos_s])),
            "maskab": mab.reshape(128, 512).astype(NPBF), "masksp": msp.astype(NPBF),
            "halov": np.broadcast_to(hv[None, :], (128, 34)).astype(f32).copy(),
            "idxown": idxown, "tbcol": (np.arange(128) % 32).astype(f32).reshape(128, 1), "ptab": page_table[16 * k:16 * k + 16].reshape(1, -1).copy(),
            "st_re": A(state_s5_re)[0, 16 * k:16 * k + 16].reshape(16, 2048).copy(),
            "st_im": A(state_s5_im)[0, 16 * k:16 * k + 16].reshape(16, 2048).copy(),
            "st_conv": A(state_conv)[0, 16 * k:16 * k + 16].reshape(32, DFF).copy(),
        })
        in_maps.append(m)

    return in_maps, qbs


def assemble(R, qbs, cores=range(8)):
    f32 = np.float32
    y_prompt = np.zeros((4, 4096, D), f32)
    y_sample = np.zeros((128, 8, D), f32)
    ckv_p = np.zeros((1, 4, T, KVR), f32); kr_p = np.zeros((1, 4, T, ROPE), f32)
    s5re_p = np.zeros((1, 4, 32, 64), f32); s5im_p = np.zeros((1, 4, 32, 64), f32)
    conv_p = np.zeros((1, 4, 2, DFF), f32)
    ckv_s = np.zeros((1, 128, 8, KVR), f32); kr_s = np.zeros((1, 128, 8, ROPE), f32)
    s5re_s = np.zeros((1, 128, 32, 64), f32); s5im_s = np.zeros((1, 128, 32, 64), f32)
    conv_s = np.zeros((1, 128, 2, DFF), f32)
    for k in cores:
        b, r = k // 2, k % 2
        o = R[k]
        oy = o["o_y"]
        for s in range(NSLOT):
            t0 = 128 * qbs[k][s]
            lo = max(t0, 16)
            y_prompt[b, lo - 16:t0 + 128 - 16] = oy[s * 128 + (lo - t0):(s + 1) * 128]
        if r == 0:
            y_prompt[b, 4080:4096] = oy[2048 + 34:2048 + 50]
            ckv_p[0, b] = o["o_ckv"][:T]; kr_p[0, b] = o["o_kr"][:T]
            s5re_p[0, b] = o["o_s5p"][0:16].reshape(32, 64); s5im_p[0, b] = o["o_s5p"][16:32].reshape(32, 64)
            conv_p[0, b] = o["o_cvp"]
        sl = slice(16 * k, 16 * k + 16)
        y_sample[sl] = o["o_ys"].reshape(16, 8, D)
        ckv_s[0, sl] = o["o_ckv"][33 * 128:].reshape(16, 8, KVR)
        kr_s[0, sl] = o["o_kr"][33 * 128:].reshape(16, 8, ROPE)
        s5re_s[0, sl] = o["o_s5s"][0:16].reshape(16, 32, 64)
        s5im_s[0, sl] = o["o_s5s"][16:32].reshape(16, 32, 64)
        conv_s[0, sl] = o["o_cvs"].reshape(16, 2, DFF)
    return (y_prompt, y_sample, ckv_p, kr_p, s5re_p, s5im_p, conv_p, ckv_s, kr_s, s5re_s, s5im_s, conv_s)


def kernel(**inputs):
    in_maps, qbs = prepare(**inputs)
    if "nc" not in _NC_CACHE:
        _NC_CACHE["nc"] = build_program()
    nc = _NC_CACHE["nc"]
    res = run_bass_kernel_spmd(nc, in_maps, core_ids=list(range(8)))
    return assemble(res.results, qbs)
```
